# Optimizing a Trainium2 kernel written in Bass

```python
import jax, jax.numpy as jnp
from jax import lax
import numpy as np

D_MODEL = 1024
BATCH = 8
SEQ = 4096
DEPTH = 4

N_HEADS = 8
QK_NOPE = 64
QK_ROPE = 32
QK_HEAD = QK_NOPE + QK_ROPE
V_HEAD = 64
Q_LORA = 384
KV_LORA = 256
ATTN_WIDTH = N_HEADS * V_HEAD
CONV_WIDTH = D_MODEL - ATTN_WIDTH
CONV_TAPS = 3
IN_COLS = Q_LORA + KV_LORA + QK_ROPE + 3 * CONV_WIDTH
D_FF = 4 * D_MODEL
PLE_DIM = 256
ROPE_THETA = 10000.0
Q_BLOCK = 128
EPS = 1e-6
MAX_POS_OFFSET = 1024

kernel_name = "hybrid_mla_shortconv_trunk"


def rmsnorm(x, g):
    xf = x.astype(jnp.float32)
    y = xf * lax.rsqrt(jnp.mean(xf * xf, axis=-1, keepdims=True) + EPS)
    return (y * g.astype(jnp.float32)).astype(x.dtype)


def rope_tables(positions):
    inv_freq = 1.0 / (ROPE_THETA ** (jnp.arange(0, QK_ROPE, 2, dtype=jnp.float32) / QK_ROPE))
    ang = positions.astype(jnp.float32)[..., None] * inv_freq
    return jnp.cos(ang)[:, :, None, :], jnp.sin(ang)[:, :, None, :]


def apply_rope(x, cos, sin):
    half = QK_ROPE // 2
    x1 = x[..., :half].astype(jnp.float32)
    x2 = x[..., half:].astype(jnp.float32)
    return jnp.concatenate([x1 * cos - x2 * sin, x2 * cos + x1 * sin], axis=-1).astype(x.dtype)


def causal_block_attention(q, k, v):
    b, s = q.shape[0], q.shape[1]
    scale = QK_HEAD ** -0.5
    n_blocks = s // Q_BLOCK
    k_idx = jnp.arange(s)

    def one_block(i):
        start = i * Q_BLOCK
        qb = lax.dynamic_slice_in_dim(q, start, Q_BLOCK, axis=1)
        sc = jnp.einsum('bqhd,bkhd->bhqk', qb, k, preferred_element_type=jnp.float32) * scale
        q_idx = start + jnp.arange(Q_BLOCK)
        sc = jnp.where(k_idx[None, :] <= q_idx[:, None], sc, -jnp.inf)
        pr = jax.nn.softmax(sc, axis=-1).astype(v.dtype)
        return jnp.einsum('bhqk,bkhd->bqhd', pr, v)

    out = lax.map(one_block, jnp.arange(n_blocks))
    return jnp.moveaxis(out, 0, 1).reshape(b, s, N_HEADS * V_HEAD)


def mla_group(q_lat, kv_lat, k_pe, cos, sin, g_q_lat, w_uq, g_kv_lat, w_ukv,
              g_qn_nope, g_qn_rope, g_kn_nope, g_kn_rope):
    b, s = q_lat.shape[0], q_lat.shape[1]
    q = (rmsnorm(q_lat, g_q_lat) @ w_uq).reshape(b, s, N_HEADS, QK_HEAD)
    kv = (rmsnorm(kv_lat, g_kv_lat) @ w_ukv).reshape(b, s, N_HEADS, QK_NOPE + V_HEAD)
    k_nope, v = kv[..., :QK_NOPE], kv[..., QK_NOPE:]
    q_nope = rmsnorm(q[..., :QK_NOPE], g_qn_nope)
    q_pe = apply_rope(rmsnorm(q[..., QK_NOPE:], g_qn_rope), cos, sin)
    k_nope = rmsnorm(k_nope, g_kn_nope)
    k_pe = apply_rope(rmsnorm(k_pe.reshape(b, s, 1, QK_ROPE), g_kn_rope), cos, sin)
    qf = jnp.concatenate([q_nope, q_pe], axis=-1)
    kf = jnp.concatenate([k_nope, jnp.broadcast_to(k_pe, (b, s, N_HEADS, QK_ROPE))], axis=-1)
    return causal_block_attention(qf, kf, v)


def short_conv_group(gate_b, gate_c, x_in, conv_w):
    u = gate_c * x_in
    s = u.shape[1]
    up = jnp.pad(u, ((0, 0), (CONV_TAPS - 1, 0), (0, 0)))
    y = sum(conv_w[j] * up[:, CONV_TAPS - 1 - j: CONV_TAPS - 1 - j + s] for j in range(CONV_TAPS))
    return gate_b * y


def setup_inputs(seed: int = 0) -> dict:
    key = jax.random.key(seed)
    ks = jax.random.split(key, 24)

    def w(k, shape, fan_in):
        return jax.random.normal(k, (DEPTH,) + shape, jnp.float32) * fan_in ** -0.5

    def gain(k, n):
        return 1.0 + 0.02 * jax.random.normal(k, (DEPTH, n), jnp.float32)

    x = jax.random.normal(ks[0], (BATCH, SEQ, D_MODEL), jnp.float32)
    p = jax.random.normal(ks[1], (DEPTH, BATCH, SEQ, PLE_DIM), jnp.float32)
    offs = jax.random.randint(ks[2], (BATCH, 1), 0, MAX_POS_OFFSET, dtype=jnp.int32)
    positions = (offs + jnp.arange(SEQ, dtype=jnp.int32)[None, :]).astype(jnp.int32)
    return {
        "x": x,
        "p": p,
        "positions": positions,
        "g_mix": gain(ks[3], D_MODEL),
        "w_in": w(ks[4], (D_MODEL, IN_COLS), D_MODEL),
        "g_q_lat": gain(ks[5], Q_LORA),
        "w_uq": w(ks[6], (Q_LORA, N_HEADS * QK_HEAD), Q_LORA),
        "g_kv_lat": gain(ks[7], KV_LORA),
        "w_ukv": w(ks[8], (KV_LORA, N_HEADS * (QK_NOPE + V_HEAD)), KV_LORA),
        "g_qn_nope": gain(ks[9], QK_NOPE),
        "g_qn_rope": gain(ks[10], QK_ROPE),
        "g_kn_nope": gain(ks[11], QK_NOPE),
        "g_kn_rope": gain(ks[12], QK_ROPE),
        "conv_w": w(ks[13], (CONV_TAPS, CONV_WIDTH), CONV_TAPS),
        "g_out_attn": gain(ks[14], ATTN_WIDTH),
        "g_out_conv": gain(ks[15], CONV_WIDTH),
        "w_o": w(ks[16], (D_MODEL, D_MODEL), D_MODEL),
        "g_mlp": gain(ks[17], D_MODEL),
        "w_up": w(ks[18], (D_MODEL, D_FF), D_MODEL),
        "w_down": w(ks[19], (D_FF, D_MODEL), D_FF),
        "g_ple": gain(ks[20], D_MODEL),
        "w_ple_gate": w(ks[21], (D_MODEL, D_MODEL), D_MODEL),
        "w_ple": w(ks[22], (PLE_DIM, D_MODEL), PLE_DIM),
    }


def reference(x, p, positions, g_mix, w_in, g_q_lat, w_uq, g_kv_lat, w_ukv,
              g_qn_nope, g_qn_rope, g_kn_nope, g_kn_rope, conv_w, g_out_attn,
              g_out_conv, w_o, g_mlp, w_up, w_down, g_ple, w_ple_gate, w_ple):
    cos, sin = rope_tables(positions)
    o1 = Q_LORA
    o2 = o1 + KV_LORA
    o3 = o2 + QK_ROPE
    o4 = o3 + CONV_WIDTH
    o5 = o4 + CONV_WIDTH
    for i in range(DEPTH):
        h = rmsnorm(x, g_mix[i])
        z = h @ w_in[i]
        attn = mla_group(z[..., :o1], z[..., o1:o2], z[..., o2:o3], cos, sin,
                         g_q_lat[i], w_uq[i], g_kv_lat[i], w_ukv[i],
                         g_qn_nope[i], g_qn_rope[i], g_kn_nope[i], g_kn_rope[i])
        conv = short_conv_group(z[..., o3:o4], z[..., o4:o5], z[..., o5:], conv_w[i])
        mixed = jnp.concatenate([rmsnorm(attn, g_out_attn[i]), rmsnorm(conv, g_out_conv[i])], axis=-1)
        x = x + mixed @ w_o[i]
        h2 = rmsnorm(x, g_mlp[i])
        x = x + jnp.square(jax.nn.relu(h2 @ w_up[i])) @ w_down[i]
        gate = jax.nn.sigmoid(rmsnorm(x, g_ple[i]) @ w_ple_gate[i])
        x = x + gate * (p[i] @ w_ple[i])
    return x
```

```python
import contextlib
import numpy as np
import concourse.bass as bass
import concourse.mybir as mybir
from concourse.bass_utils import run_bass_kernel_spmd

F32 = mybir.dt.float32
BF16 = mybir.dt.bfloat16
I32 = mybir.dt.int32
ALU = mybir.AluOpType
AF = mybir.ActivationFunctionType

S = 4096
D = 1024
NL = 4
TB = 512
NB = S // TB
TA = 512
NBA = S // TA
WCH = 6
K_HP = 2
K_KS = 2
K_LA = 4
K_STEP = 4
RAB = TB // TA
EPS = 1e-6
INC = 2208
NPIECE = 21
PIECE = 4096

G_MIX, G_MLP, G_PLE, G_QL, G_KVL, G_OC, G_CW, G_OA, G_QN, G_KN2, G_KR = range(11)
_GW = [8, 8, 8, 3, 2, 4, 12, 8, 1, 1, 1]
_GOFF = np.concatenate([[0], np.cumsum([w * NL for w in _GW])]).astype(int)
NG = int(_GOFF[-1])


def gcol(kind, l, i=0):
    return int(_GOFF[kind] + l * _GW[kind] + i)


C_ONES, C_BLK96, C_BLK2, C_ROT, C_TRI = 0, 128, 256, 384, 512
C_INVF = 640
NCM = 641


class Op:
    __slots__ = ("eng", "fn", "kind", "deps", "signal", "sem", "sigval", "waits", "idx")

    def __init__(self, eng, fn, kind):
        self.eng = eng
        self.fn = fn
        self.kind = kind
        self.deps = []
        self.signal = False
        self.sem = None
        self.sigval = 0
        self.waits = []


SAME_ENG_FULL = ("pool", "dve", "act")


class Prog:
    ENGS = ("pe", "act", "dve", "pool", "sp")
    NSLOT = {"pe": 2, "act": 2, "dve": 2, "pool": 64, "sp": 24}

    def __init__(self, nc):
        self.nc = nc
        self.ops = []
        self.last_writer = {}
        self.readers = {}

    def _add(self, eng, fn, kind, reads, writes):
        op = Op(eng, fn, kind)
        op.idx = len(self.ops)
        deps = {}
        rk = []
        for r in reads:
            rk.extend(r)
        wk = []
        for w in writes:
            wk.extend(w)
        for k in rk:
            w = self.last_writer.get(k)
            if w is not None:
                deps[w.idx] = (w, True)
        for k in wk:
            w = self.last_writer.get(k)
            if w is not None and w.idx not in deps:
                deps[w.idx] = (w, False)
            for r in self.readers.get(k, ()):
                if r.idx not in deps:
                    deps[r.idx] = (r, False)
        for (d, raw) in deps.values():
            if d.kind == "c" and kind == "c" and d.eng == eng:
                if eng == "pe" or (not raw and eng not in SAME_ENG_FULL):
                    continue
            op.deps.append(d)
            d.signal = True
        for k in wk:
            self.last_writer[k] = op
            self.readers[k] = []
        for k in rk:
            self.readers.setdefault(k, []).append(op)
        self.ops.append(op)
        return op

    def c(self, eng, fn, reads=(), writes=()):
        return self._add(eng, fn, "c", reads, writes)

    def dma(self, eng, fn, reads=(), writes=()):
        op = self._add(eng, fn, "d", reads, writes)
        op.signal = True
        return op

    def emit(self, final_ops=(), final_eng="sp"):
        nc = self.nc
        with contextlib.ExitStack() as st:
            csem = {e: st.enter_context(nc.semaphore("c_" + e)) for e in self.ENGS}
            dsem = {e: [st.enter_context(nc.semaphore("d_%s_%d" % (e, i))) for i in range(self.NSLOT[e])]
                    for e in self.ENGS}
            ccount = {e: 0 for e in self.ENGS}
            dcount = {e: 0 for e in self.ENGS}
            slotlast = {e: [None] * self.NSLOT[e] for e in self.ENGS}
            slotcnt = {e: [0] * self.NSLOT[e] for e in self.ENGS}
            waited = {e: {} for e in self.ENGS}
            for op in self.ops:
                w = waited[op.eng]
                waits = []

                def need(sem, val):
                    if w.get(id(sem), 0) < val:
                        w[id(sem)] = val
                        waits.append((sem, val))

                for d in op.deps:
                    need(d.sem, d.sigval)
                if op.kind == "d":
                    e = op.eng
                    s = dcount[e] % self.NSLOT[e]
                    dcount[e] += 1
                    prev = slotlast[e][s]
                    if prev is not None:
                        need(prev.sem, prev.sigval)
                    slotcnt[e][s] += 1
                    op.sem = dsem[e][s]
                    op.sigval = 16 * slotcnt[e][s]
                    slotlast[e][s] = op
                elif op.signal:
                    ccount[op.eng] += 1
                    op.sem = csem[op.eng]
                    op.sigval = ccount[op.eng]
                op.waits = waits
            fin = []
            wf = waited[final_eng]
            for d in final_ops:
                if wf.get(id(d.sem), 0) < d.sigval:
                    wf[id(d.sem)] = d.sigval
                    fin.append((d.sem, d.sigval))
            per = {e: [o for o in self.ops if o.eng == e] for e in self.ENGS}
            self.stats = {e: len(per[e]) for e in self.ENGS}
            self.stats["sig"] = dict(ccount)

            def run(eng_obj, e):
                for op in per[e]:
                    for (sem, val) in op.waits:
                        eng_obj.wait_ge(sem, val)
                    ins = op.fn(eng_obj)
                    if op.signal:
                        ins.then_inc(op.sem, 16 if op.kind == "d" else 1)
                if e == final_eng:
                    for (sem, val) in fin:
                        eng_obj.wait_ge(sem, val)

            with nc.Block() as block:
                @block.tensor
                def _(eng):
                    run(eng, "pe")

                @block.scalar
                def _(eng):
                    run(eng, "act")

                @block.vector
                def _(eng):
                    run(eng, "dve")

                @block.gpsimd
                def _(eng):
                    run(eng, "pool")

                @block.sync
                def _(eng):
                    run(eng, "sp")


PAGE = 256


def mk(method, *args, **kw):
    return lambda e: getattr(e, method)(*args, **kw)


class Reg:
    def __init__(self, ap, lo, hi):
        self.ap = ap
        self.lo = lo
        self.hi = hi
        self.keys = list(range(lo // PAGE, (hi - 1) // PAGE + 1))

    def __iter__(self):
        return iter(self.keys)

    def sub(self, i, n=1):
        nfirst = self.ap.shape[1]
        step = (self.hi - self.lo) // nfirst
        ap = self.ap[:, i] if n == 1 else self.ap[:, i:i + n]
        return Reg(ap, self.lo + i * step, self.lo + (i + n) * step)


class Arena:
    def __init__(self, tile, nbytes):
        self.t = tile
        self.n = nbytes
        self.off = 0
        self.marks = []

    def alloc(self, dtype, free, parts=128):
        es = 4 if dtype in (F32, I32) else 2
        n = es
        for f in free:
            n *= f
        lo = (self.off + 255) // 256 * 256
        hi = lo + n
        assert hi <= self.n, "arena overflow %d > %d" % (hi, self.n)
        self.off = hi
        ap = self.t[0:parts, lo // 2:hi // 2]
        if dtype != BF16:
            ap = ap.bitcast(dtype)
        if len(free) == 2:
            ap = ap.rearrange("p (a b) -> p a b", b=free[1])
        elif len(free) == 3:
            ap = ap.rearrange("p (a b c) -> p a b c", b=free[1], c=free[2])
        return Reg(ap, lo, hi)

    def mark(self):
        return self.off

    def reset(self, m):
        self.off = m


def build(nl=NL, dbg=False):
    nc = bass.Bass("TRN2", target_bir_lowering=False)
    dt_in = lambda name, shape, dt=F32: nc.dram_tensor(name, shape, dt, kind="ExternalInput").ap()
    xT = dt_in("xT", [D, S])
    pT = dt_in("pT", [NL, 256, S])
    pos = dt_in("pos", [1, S], I32)
    w_in = dt_in("w_in", [NL, D, INC])
    w_uq = dt_in("w_uq", [NL, 384, 768])
    w_ukv = dt_in("w_ukv", [NL, 256, 1024])
    w_o = dt_in("w_o", [NL, D, D])
    w_up = dt_in("w_up", [NL, D, 4096])
    w_down = dt_in("w_down", [NL, 4096, D])
    w_g = dt_in("w_ple_gate", [NL, D, D])
    w_ple = dt_in("w_ple", [NL, 256, D])
    gpack = dt_in("gpack", [128, NG])
    cmat = dt_in("cmat", [128, NCM])
    outT = nc.dram_tensor("outT", [D, S], F32, kind="ExternalOutput").ap()

    WIN_B = nc.dram_tensor("WIN_B", [NL, 128, 8 * INC], BF16).ap()
    WK_B = nc.dram_tensor("WK_B", [NL, 128, 1024], BF16).ap()
    WV_B = nc.dram_tensor("WV_B", [NL, 128, 1024], BF16).ap()
    WUQ_B = nc.dram_tensor("WUQ_B", [NL, 128, 3 * 768], BF16).ap()
    WC_B = nc.dram_tensor("WC_B", [NL, NPIECE, 128, PIECE], BF16).ap()
    sk = "ExternalOutput" if dbg else "Internal"
    QTD = nc.dram_tensor("QTD", [96, 8, S], BF16, kind=sk).ap()
    KTD = nc.dram_tensor("KTD", [96, 8, S], BF16, kind=sk).ap()
    VVD = nc.dram_tensor("VVD", [128, 32, 8 * 65], BF16, kind=sk).ap()
    MIXA = nc.dram_tensor("MIXA", [128, 4, S], BF16, kind=sk).ap()
    MIXC = nc.dram_tensor("MIXC", [128, 4, S], BF16, kind=sk).ap()
    COSD = nc.dram_tensor("COSD", [96, S], F32).ap()
    SIND = nc.dram_tensor("SIND", [96, S], F32).ap()

    ARENA_BYTES = 204 * 1024
    with contextlib.ExitStack() as st:
        arena_t = st.enter_context(nc.sbuf_tensor("arena", [128, ARENA_BYTES // 2], BF16))
        pst = [st.enter_context(nc.psum_tensor("ps%d" % i, [128, 512], F32)) for i in range(8)]
        A = Arena(arena_t, ARENA_BYTES)
        P = Prog(nc)
        psk = [[("ps", i)] for i in range(8)]
        psctr = [0]

        held = set()

        def nextps(hold=False):
            while True:
                i = psctr[0] % 8
                psctr[0] += 1
                if i not in held:
                    break
            if hold:
                held.add(i)
            return pst[i], psk[i]

        def release(pk):
            held.discard(pk[0][1])

        def dk(name, *idx):
            return [("d", name) + tuple(idx)]

        gp = A.alloc(F32, [NG])
        cm_f = A.alloc(F32, [NCM])
        cm_b = A.alloc(BF16, [640])
        epsc = A.alloc(F32, [1])
        P.dma("sp", mk("dma_start", out=gp.ap, in_=gpack), writes=[gp])
        P.dma("sp", mk("dma_start", out=cm_f.ap, in_=cmat), writes=[cm_f])
        P.c("dve", mk("tensor_copy", out=cm_b.ap, in_=cm_f.ap[:, 0:640]), reads=[cm_f], writes=[cm_b])
        P.c("dve", mk("memset", epsc.ap, EPS), writes=[epsc])
        ONES = cm_b.ap[:, C_ONES:C_ONES + 128]
        BLK96 = cm_b.ap[0:96, C_BLK96:C_BLK96 + 96]
        BLK2 = cm_b.ap[:, C_BLK2:C_BLK2 + 128]
        ROT = cm_b.ap[0:96, C_ROT:C_ROT + 96]
        TRI = cm_b.ap[:, C_TRI:C_TRI + 128]
        ONESF = cm_f.ap[:, C_ONES:C_ONES + 64]

        def G(kind, l, i=0, lo=0, hi=128):
            c = gcol(kind, l, i)
            return gp.ap[lo:hi, c:c + 1]

        cast_q = []
        defer_casts = [False]

        def pdma_cast(fn, writes):
            if defer_casts[0]:
                cast_q.append((fn, writes))
            else:
                P.dma("pool", fn, writes=writes)

        def flush_casts(n):
            for _ in range(min(n, len(cast_q))):
                fn, writes = cast_q.pop(0)
                P.dma("pool", fn, writes=writes)

        def emit_casts(l, part):
            if part == "A":
                emit_casts_a(l)
            else:
                emit_casts_c(l)

        def emit_casts_a(l):
            v = w_in[l].rearrange("(kc p) n -> p kc n", p=128)
            dv = WIN_B[l].rearrange("p (kc n) -> p kc n", n=INC)
            for kc in range(8):
                pdma_cast(mk("dma_start", out=dv[:, kc], in_=v[:, kc]), [dk("WIN", l, kc)])
            v = w_ukv[l].rearrange("(kc p) (h x) -> p kc h x", p=128, x=128)
            for kc in range(2):
                pdma_cast(mk("dma_start",
                    out=WK_B[l].rearrange("p (kc h x) -> p kc h x", kc=2, x=64)[:, kc], in_=v[:, kc, :, 0:64]), [dk("WK", l, kc)])
                pdma_cast(mk("dma_start",
                    out=WV_B[l].rearrange("p (kc h x) -> p kc h x", kc=2, x=64)[:, kc], in_=v[:, kc, :, 64:128]), [dk("WV", l, kc)])
            pdma_cast(mk("dma_start",
                out=WUQ_B[l].rearrange("p (kc n) -> p kc n", n=768),
                in_=w_uq[l].rearrange("(kc p) n -> p kc n", p=128)), [dk("WUQ", l)])

        def emit_casts_c(l):
            for i in range(2):
                pdma_cast(mk("dma_start",
                    out=WC_B[l, i].rearrange("p (c n) -> p c n", n=1024),
                    in_=w_o[l][512 * i:512 * (i + 1)].rearrange("(c p) n -> p c n", p=128)), [dk("WC", l, i)])
            vu = w_up[l].rearrange("(kc p) n -> p kc n", p=128)
            for g in range(8):
                pdma_cast(mk("dma_start",
                    out=WC_B[l, 2 + g].rearrange("p (kc n) -> p kc n", n=512), in_=vu[:, :, g * 512:(g + 1) * 512]), [dk("WC", l, 2 + g)])
            vd = w_down[l].rearrange("(fc p) n -> p fc n", p=128)
            for oc in range(8):
                pdma_cast(mk("dma_start",
                    out=WC_B[l, 10 + oc].rearrange("p (fc n) -> p fc n", n=128), in_=vd[:, :, oc * 128:(oc + 1) * 128]), [dk("WC", l, 10 + oc)])
            vg = w_g[l].rearrange("(kc p) n -> p kc n", p=128)
            for i in range(2):
                pdma_cast(mk("dma_start",
                    out=WC_B[l, 18 + i].rearrange("p (kc n) -> p kc n", n=512), in_=vg[:, :, i * 512:(i + 1) * 512]), [dk("WC", l, 18 + i)])
            pdma_cast(mk("dma_start",
                out=WC_B[l, 20][:, 0:2048].rearrange("p (kc n) -> p kc n", n=1024),
                in_=w_ple[l].rearrange("(kc p) n -> p kc n", p=128)), [dk("WC", l, 20)])

        emit_casts(0, "A")

        m0 = A.mark()
        COS = A.alloc(F32, [S], parts=96)
        SIN = A.alloc(F32, [S], parts=96)
        posi = A.alloc(I32, [S], parts=96)
        ang = A.alloc(F32, [S], parts=96)
        kk = A.alloc(F32, [S], parts=96)
        ki = A.alloc(I32, [S], parts=96)
        P.dma("sp", mk("dma_start", out=posi.ap, in_=pos.partition_broadcast(96)), writes=[posi])
        P.c("dve", mk("tensor_copy", out=ang.ap, in_=posi.ap), reads=[posi], writes=[ang])
        INVF = cm_f.ap[0:96, C_INVF:C_INVF + 1]
        P.c("dve", mk("tensor_scalar", out=ang.ap, in0=ang.ap, scalar1=INVF, scalar2=None, op0=ALU.mult),
            reads=[ang, cm_f], writes=[ang])
        TWO_PI = 2.0 * np.pi
        C1 = 6.28125
        C2 = TWO_PI - C1
        for (tab, shift) in ((SIN, 0.0), (COS, np.pi / 2)):
            P.c("dve", mk("tensor_scalar",
                out=tab.ap, in0=ang.ap, scalar1=float(shift), scalar2=None, op0=ALU.add), reads=[ang], writes=[tab])
            P.c("dve", mk("tensor_scalar",
                out=ki.ap, in0=tab.ap, scalar1=1.0 / TWO_PI, scalar2=None, op0=ALU.mult), reads=[tab], writes=[ki])
            P.c("dve", mk("tensor_copy", out=kk.ap, in_=ki.ap), reads=[ki], writes=[kk])
            P.c("dve", mk("scalar_tensor_tensor",
                out=tab.ap, in0=kk.ap, scalar=-C1, in1=tab.ap, op0=ALU.mult, op1=ALU.add), reads=[kk, tab], writes=[tab])
            P.c("dve", mk("scalar_tensor_tensor",
                out=tab.ap, in0=kk.ap, scalar=-C2, in1=tab.ap, op0=ALU.mult, op1=ALU.add), reads=[kk, tab], writes=[tab])
            P.c("dve", mk("tensor_scalar",
                out=tab.ap, in0=tab.ap, scalar1=-np.pi, scalar2=np.pi, op0=ALU.max, op1=ALU.min), reads=[tab], writes=[tab])
            P.c("act", mk("activation", out=tab.ap, in_=tab.ap, func=AF.Sin), reads=[tab], writes=[tab])
        P.dma("sp", mk("dma_start", out=COSD, in_=COS.ap), reads=[COS], writes=[dk("COSD")])
        P.dma("sp", mk("dma_start", out=SIND, in_=SIN.ap), reads=[SIN], writes=[dk("SIND")])
        A.reset(m0)
        const_mark = A.mark()

        def rms_rstd(ssq_ps, ssq_k, scale, parts, tmp, rstd, n=TB):
            P.c("act", mk("activation", out=tmp.ap[0:parts, 0:n], in_=ssq_ps[0:parts, 0:n], func=AF.Ln,
                                              bias=epsc.ap[0:parts], scale=float(scale)),
                reads=[ssq_k, epsc], writes=[tmp])
            P.c("act", mk("activation", out=rstd.ap[0:parts, 0:n], in_=tmp.ap[0:parts, 0:n], func=AF.Exp, scale=-0.5),
                reads=[tmp], writes=[rstd])

        def norm_block(xb, nch, gkind, l, scale, sqb, hb, tmp, rstd):
            ps, pk = nextps()
            for c in range(nch):
                sq = sqb[c % len(sqb)]
                P.c("act", mk("activation", out=sq.ap, in_=xb.ap[:, c], func=AF.Square),
                    reads=[xb.sub(c)], writes=[sq])
                P.c("pe", mk("matmul", ps[:, :], lhsT=ONES, rhs=sq.ap, start=(c == 0), stop=(c == nch - 1)),
                    reads=[sq, cm_b], writes=[pk])
            rms_rstd(ps, pk, scale, 128, tmp, rstd)
            for c in range(nch):
                P.c("dve", mk("scalar_tensor_tensor",
                    out=hb.ap[:, c], in0=xb.ap[:, c], scalar=G(gkind, l, c), in1=rstd.ap, op0=ALU.mult, op1=ALU.mult),
                    reads=[xb.sub(c), rstd, gp], writes=[hb.sub(c)])

        def head_post_gen(raw_ps, raw_k, gcolap, cs_regs, dst_ap, dst_reg, T, held_raw=False, n=TB):
            sq, qn, t1, t2, tr = T
            cosb, sinb = cs_regs
            P.c("act", mk("activation", out=sq.ap[0:96, 0:n], in_=raw_ps[0:96, 0:n], func=AF.Square),
                reads=[raw_k], writes=[sq])
            yield
            ps2, pk2 = nextps()
            P.c("pe", mk("matmul", ps2[0:96, 0:n], lhsT=BLK96, rhs=sq.ap[0:96, 0:n], start=True, stop=True),
                reads=[sq, cm_b], writes=[pk2])
            rms_rstd(ps2, pk2, 1.0, 96, tr, tr, n)
            P.c("dve", mk("scalar_tensor_tensor", out=qn.ap[0:96, 0:n], in0=raw_ps[0:96, 0:n], scalar=gcolap,
                                                        in1=tr.ap[0:96, 0:n], op0=ALU.mult, op1=ALU.mult),
                reads=[raw_k, tr, gp], writes=[qn])
            if held_raw:
                release(raw_k)
            yield
            ps3, pk3 = nextps()
            P.c("pe", mk("matmul", ps3[0:96, 0:n], lhsT=ROT, rhs=qn.ap[0:96, 0:n], start=True, stop=True),
                reads=[qn, cm_b], writes=[pk3])
            P.c("dve", mk("tensor_tensor", out=t1.ap[0:96, 0:n], in0=qn.ap[0:96, 0:n], in1=cosb.ap[0:96, 0:n], op=ALU.mult),
                reads=[qn, cosb], writes=[t1])
            P.c("dve", mk("tensor_tensor", out=t2.ap[0:96, 0:n], in0=ps3[0:96, 0:n], in1=sinb.ap[0:96, 0:n], op=ALU.mult),
                reads=[pk3, sinb], writes=[t2])
            P.c("pool", mk("tensor_tensor", out=dst_ap, in0=t1.ap[0:96, 0:n], in1=t2.ap[0:96, 0:n], op=ALU.add),
                reads=[t1, t2], writes=[dst_reg])

        def head_post(*a):
            for _ in head_post_gen(*a):
                pass

        bg = []

        def step():
            if bg:
                try:
                    next(bg[0])
                except StopIteration:
                    bg.pop(0)

        def drain():
            while bg:
                step()

        def run_chains(chains, width, tokens=None):
            tokens = {k: list(v) for k, v in (tokens or {}).items()}
            active = []
            pending = list(chains)
            finished = set()
            while active or pending:
                i = 0
                while i < len(pending) and len(active) < width:
                    c = pending[i]
                    if isinstance(c, tuple):
                        name, cls, fac, after = c
                        ok = all(a in finished for a in after) and (cls is None or tokens[cls])
                        if not ok:
                            i += 1
                            continue
                        tok = tokens[cls].pop(0) if cls is not None else None
                        active.append((fac(tok), cls, tok, name))
                    else:
                        active.append((c, None, None, None))
                    pending.pop(i)
                assert active, "chain scheduler deadlock"
                for ent in list(active):
                    g, cls, tok, name = ent
                    try:
                        next(g)
                    except StopIteration:
                        active.remove(ent)
                        if cls is not None:
                            tokens[cls].append(tok)
                        if name is not None:
                            finished.add(name)

        def xsrc(l):
            return xT if l == 0 else outT

        def xview(ap, t):
            return ap.rearrange("(c p) t -> p c t", p=128)[:, :, t * TB:(t + 1) * TB]

        out_stores = []

        for l in range(nl):
            A.reset(const_mark)
            w_in_s = A.alloc(BF16, [8, INC])
            wk_s = A.alloc(BF16, [2, 512])
            wv_s = A.alloc(BF16, [2, 512])
            wuq_s = A.alloc(BF16, [3, 768])
            xb1 = A.alloc(F32, [8, TA])
            hb2 = [A.alloc(BF16, [8, TA]) for _ in range(2)]
            sqn = A.alloc(BF16, [8, TA])
            trn = A.alloc(F32, [TA])
            zlat = A.alloc(F32, [5, TA])
            sql = A.alloc(BF16, [5, TA])
            trl = [A.alloc(F32, [TA]) for _ in range(2)]
            qln_b = A.alloc(BF16, [3, TA])
            kvn_b = A.alloc(BF16, [2, TA])
            hpT = [[A.alloc(BF16, [TA]), A.alloc(BF16, [TA]), A.alloc(F32, [TA]), A.alloc(F32, [TA]),
                    A.alloc(F32, [TA])] for _ in range(2)]
            kpe_f = A.alloc(BF16, [TA])
            qblk = A.alloc(BF16, [8, TA], parts=96)
            cs1 = A.alloc(F32, [TA])
            bs1 = A.alloc(F32, [TA])
            ubuf = [A.alloc(F32, [TA + 2]) for _ in range(4)]
            cacc1 = A.alloc(F32, [TA])
            conv_u = A.alloc(F32, [4, TA])
            conv_n = A.alloc(BF16, [4, TA])
            trc = A.alloc(F32, [TA])
            ksq2 = [A.alloc(BF16, [TA]) for _ in range(2)]
            ktr2 = [A.alloc(F32, [TA]) for _ in range(2)]
            ktb = A.alloc(BF16, [8, TA], parts=96)
            vb = A.alloc(BF16, [TA // 128, 8 * 65])
            cosb2 = [A.alloc(F32, [TA], parts=96) for _ in range(2)]
            sinb2 = [A.alloc(F32, [TA], parts=96) for _ in range(2)]

            P.dma("sp", mk("dma_start", out=w_in_s.ap, in_=WIN_B[l].rearrange("p (kc n) -> p kc n", n=INC)),
                  reads=[dk("WIN", l, kc) for kc in range(8)], writes=[w_in_s])
            P.dma("sp", mk("dma_start", out=wk_s.ap, in_=WK_B[l].rearrange("p (kc n) -> p kc n", n=512)),
                  reads=[dk("WK", l, 0), dk("WK", l, 1)], writes=[wk_s])
            P.dma("sp", mk("dma_start", out=wv_s.ap, in_=WV_B[l].rearrange("p (kc n) -> p kc n", n=512)),
                  reads=[dk("WV", l, 0), dk("WV", l, 1)], writes=[wv_s])
            P.dma("sp", mk("dma_start", out=wuq_s.ap, in_=WUQ_B[l].rearrange("p (kc n) -> p kc n", n=768)),
                  reads=[dk("WUQ", l)], writes=[wuq_s])
            vb4 = vb.ap.rearrange("p a (h x) -> p a h x", x=65)
            P.c("pool", mk("memset", vb.ap, 1.0), writes=[vb])
            for cc in range(4):
                P.c("pool", mk("memset", ubuf[cc].ap[:, 0:2], 0.0), writes=[ubuf[cc]])

            def xviewA(ap, t):
                return ap.rearrange("(c p) t -> p c t", p=128)[:, :, t * TA:(t + 1) * TA]

            def loadcs(t):
                tc_ = slice(t * TA, (t + 1) * TA)
                P.dma("sp", mk("dma_start", out=cosb2[t % 2].ap, in_=COSD[:, tc_]), reads=[dk("COSD")], writes=[cosb2[t % 2]])
                P.dma("sp", mk("dma_start", out=sinb2[t % 2].ap, in_=SIND[:, tc_]), reads=[dk("SIND")], writes=[sinb2[t % 2]])

            def loadxA(t):
                xb = xb1
                P.dma("sp", mk("dma_start", out=xb.ap, in_=xviewA(xsrc(l), t)), reads=[dk("X", (t * TA) // TB)], writes=[xb])

            def c_normx(t):
                xb = xb1
                hbuf = hb2[t % 2]
                for c in range(8):
                    P.c("act", mk("activation", out=sqn.ap[:, c], in_=xb.ap[:, c], func=AF.Square),
                        reads=[xb.sub(c)], writes=[sqn.sub(c)])
                yield
                ps, pk = nextps()
                for c in range(8):
                    P.c("pe", mk("matmul", ps[:, 0:TA], lhsT=ONES, rhs=sqn.ap[:, c], start=(c == 0), stop=(c == 7)),
                        reads=[sqn.sub(c), cm_b], writes=[pk])
                rms_rstd(ps, pk, 1.0 / D, 128, trn, trn, TA)
                yield
                for c in range(8):
                    P.c("dve", mk("scalar_tensor_tensor",
                        out=hbuf.ap[:, c], in0=xb.ap[:, c], scalar=G(G_MIX, l, c), in1=trn.ap, op0=ALU.mult, op1=ALU.mult),
                        reads=[xb.sub(c), trn, gp], writes=[hbuf.sub(c)])
                if t + 1 < NBA:
                    loadxA(t + 1)

            def zmm(hbuf, ps, pk, c0, m):
                for kc in range(8):
                    P.c("pe", mk("matmul", ps[0:m, 0:TA], lhsT=w_in_s.ap[:, kc, c0:c0 + m], rhs=hbuf.ap[:, kc],
                                 start=(kc == 0), stop=(kc == 7)),
                        reads=[w_in_s, hbuf.sub(kc)], writes=[pk])

            def c_lat(hbuf, which):
                c0, nch, zoff, gk, scale, dstb = ((0, 3, 0, G_QL, 1.0 / 384, qln_b), (384, 2, 3, G_KVL, 1.0 / 256, kvn_b))[which]
                tmp = rstd = trl[which]
                for c in range(nch):
                    ps, pk = nextps()
                    zmm(hbuf, ps, pk, c0 + c * 128, 128)
                    P.c("act", mk("activation", out=zlat.ap[:, zoff + c], in_=ps[:, 0:TA], func=AF.Copy),
                        reads=[pk], writes=[zlat.sub(zoff + c)])
                    P.c("act", mk("activation", out=sql.ap[:, zoff + c], in_=ps[:, 0:TA], func=AF.Square),
                        reads=[pk], writes=[sql.sub(zoff + c)])
                    yield
                ps, pk = nextps()
                for c in range(nch):
                    P.c("pe", mk("matmul", ps[:, 0:TA], lhsT=ONES, rhs=sql.ap[:, zoff + c], start=(c == 0), stop=(c == nch - 1)),
                        reads=[sql.sub(zoff + c), cm_b], writes=[pk])
                rms_rstd(ps, pk, scale, 128, tmp, rstd, TA)
                yield
                for c in range(nch):
                    P.c("dve", mk("scalar_tensor_tensor",
                        out=dstb.ap[:, c], in0=zlat.ap[:, zoff + c], scalar=G(gk, l, c), in1=rstd.ap, op0=ALU.mult, op1=ALU.mult),
                        reads=[zlat.sub(zoff + c), rstd, gp], writes=[dstb.sub(c)])

            def c_kpe(hbuf, tcols, tok):
                ps, pk = nextps(hold=True)
                zmm(hbuf, ps, pk, 576, 96)
                yield from head_post_gen(ps, pk, G(G_KR, l, 0, 0, 96), tcols, kpe_f.ap[0:96], kpe_f, hpT[tok], True, TA)

            def c_qh(h, tcols, tok):
                ps, pk = nextps(hold=True)
                for kc in range(3):
                    P.c("pe", mk("matmul", ps[0:96, 0:TA], lhsT=wuq_s.ap[:, kc, h * 96:(h + 1) * 96], rhs=qln_b.ap[:, kc],
                                 start=(kc == 0), stop=(kc == 2)),
                        reads=[wuq_s, qln_b.sub(kc)], writes=[pk])
                yield from head_post_gen(ps, pk, G(G_QN, l, 0, 0, 96), tcols, qblk.ap[0:96, h], qblk.sub(h), hpT[tok], True, TA)

            def c_conv(hbuf, cc):
                cs, bs, cacc, ub = cs1, bs1, cacc1, ubuf[cc]
                psb, pkb = nextps()
                zmm(hbuf, psb, pkb, 672 + cc * 128, 128)
                P.c("act", mk("activation", out=bs.ap, in_=psb[:, 0:TA], func=AF.Copy), reads=[pkb], writes=[bs])
                yield
                psc, pkc = nextps()
                zmm(hbuf, psc, pkc, 1184 + cc * 128, 128)
                P.c("act", mk("activation", out=cs.ap, in_=psc[:, 0:TA], func=AF.Copy), reads=[pkc], writes=[cs])
                yield
                psx, pkx = nextps()
                zmm(hbuf, psx, pkx, 1696 + cc * 128, 128)
                P.c("dve", mk("tensor_tensor", out=ub.ap[:, 2:TA + 2], in0=cs.ap, in1=psx[:, 0:TA], op=ALU.mult),
                    reads=[cs, pkx], writes=[ub])
                yield
                P.c("act", mk("activation", out=cacc.ap, in_=ub.ap[:, 2:TA + 2], func=AF.Copy, scale=G(G_CW, l, 0 * 4 + cc)),
                    reads=[ub, gp], writes=[cacc])
                P.c("dve", mk("scalar_tensor_tensor", out=cacc.ap, in0=ub.ap[:, 1:TA + 1], scalar=G(G_CW, l, 1 * 4 + cc),
                              in1=cacc.ap, op0=ALU.mult, op1=ALU.add),
                    reads=[ub, gp, cacc], writes=[cacc])
                P.c("dve", mk("scalar_tensor_tensor", out=cacc.ap, in0=ub.ap[:, 0:TA], scalar=G(G_CW, l, 2 * 4 + cc),
                              in1=cacc.ap, op0=ALU.mult, op1=ALU.add),
                    reads=[ub, gp, cacc], writes=[cacc])
                P.c("pool", mk("tensor_tensor", out=conv_u.ap[:, cc], in0=cacc.ap, in1=bs.ap, op=ALU.mult),
                    reads=[cacc, bs], writes=[conv_u.sub(cc)])
                P.c("pool", mk("tensor_copy", out=ub.ap[:, 0:2], in_=ub.ap[:, TA:TA + 2]), reads=[ub], writes=[ub])

            def c_convnorm(t, tcols):
                for cc in range(4):
                    P.c("act", mk("activation", out=sqn.ap[:, cc], in_=conv_u.ap[:, cc], func=AF.Square),
                        reads=[conv_u.sub(cc)], writes=[sqn.sub(cc)])
                yield
                ps, pk = nextps()
                for cc in range(4):
                    P.c("pe", mk("matmul", ps[:, 0:TA], lhsT=ONES, rhs=sqn.ap[:, cc], start=(cc == 0), stop=(cc == 3)),
                        reads=[sqn.sub(cc), cm_b], writes=[pk])
                rms_rstd(ps, pk, 1.0 / 512, 128, trc, trc, TA)
                yield
                for cc in range(4):
                    P.c("dve", mk("scalar_tensor_tensor",
                        out=conv_n.ap[:, cc], in0=conv_u.ap[:, cc], scalar=G(G_OC, l, cc), in1=trc.ap, op0=ALU.mult, op1=ALU.mult),
                        reads=[conv_u.sub(cc), trc, gp], writes=[conv_n.sub(cc)])
                P.dma("sp", mk("dma_start", out=MIXC[:, :, tcols], in_=conv_n.ap), reads=[conv_n], writes=[dk("MIXC", t)])

            def c_kpair(jp, k):
                ksq = ksq2[k]
                ktmp = krstd = ktr2[k]
                ps, pk = nextps(hold=True)
                for kc in range(2):
                    P.c("pe", mk("matmul", ps[:, 0:TA], lhsT=wk_s.ap[:, kc, jp * 128:(jp + 1) * 128], rhs=kvn_b.ap[:, kc],
                                 start=(kc == 0), stop=(kc == 1)),
                        reads=[wk_s, kvn_b.sub(kc)], writes=[pk])
                P.c("act", mk("activation", out=ksq.ap, in_=ps[:, 0:TA], func=AF.Square), reads=[pk], writes=[ksq])
                yield
                ps2, pk2 = nextps()
                P.c("pe", mk("matmul", ps2[:, 0:TA], lhsT=BLK2, rhs=ksq.ap, start=True, stop=True),
                    reads=[ksq, cm_b], writes=[pk2])
                rms_rstd(ps2, pk2, 1.0, 128, ktmp, krstd, TA)
                yield
                for hh in range(2):
                    lo, hi = hh * 64, hh * 64 + 64
                    P.c("dve", mk("scalar_tensor_tensor",
                        out=ktb.ap[0:64, 2 * jp + hh], in0=ps[lo:hi, 0:TA], scalar=G(G_KN2, l, 0, lo, hi), in1=krstd.ap[lo:hi],
                        op0=ALU.mult, op1=ALU.mult),
                        reads=[pk, krstd, gp], writes=[ktb.sub(2 * jp + hh)])
                release(pk)

            def c_v(t):
                for i in range(TA // 128):
                    ps, pk = nextps()
                    for kc in range(2):
                        P.c("pe", mk("matmul", ps[:, :], lhsT=kvn_b.ap[:, kc, i * 128:(i + 1) * 128], rhs=wv_s.ap[:, kc],
                                     start=(kc == 0), stop=(kc == 1)),
                            reads=[wv_s, kvn_b.sub(kc)], writes=[pk])
                    P.c("act", mk("activation", out=vb4[:, i, :, 0:64], in_=ps[:, :].rearrange("p (h x) -> p h x", x=64), func=AF.Copy),
                        reads=[pk], writes=[vb.sub(i)])
                    yield
                nt = TA // 128
                P.dma("sp", mk("dma_start", out=VVD[:, nt * t:nt * t + nt, :], in_=vb.ap), reads=[vb], writes=[dk("VV", t)])

            def c_kfin(t, tcols):
                for h in range(8):
                    P.c("pool", mk("tensor_copy", out=ktb.ap[64:96, h], in_=kpe_f.ap[64:96]), reads=[kpe_f], writes=[ktb.sub(h)])
                P.dma("sp", mk("dma_start", out=KTD[:, :, tcols], in_=ktb.ap), reads=[ktb], writes=[dk("KT", t)])
                yield

            def c_qfin(t, tcols):
                P.dma("sp", mk("dma_start", out=QTD[:, :, tcols], in_=qblk.ap), reads=[qblk], writes=[dk("QT", t)])
                yield

            loadxA(0)
            loadcs(0)
            run_chains([c_normx(0)], 1)
            for t in range(NBA):
                tcols = (cosb2[t % 2], sinb2[t % 2])
                tsl = slice(t * TA, (t + 1) * TA)
                if t + 1 < NBA:
                    loadcs(t + 1)
                hbuf = hb2[t % 2]
                def QH(h):
                    return ("qh%d" % h, "hp", lambda tok, h=h: c_qh(h, tcols, tok), ["latq"])

                def KP(jp):
                    return ("kp%d" % jp, "ks", lambda tok, jp=jp: c_kpair(jp, tok), ["latkv"])

                def CV(cc):
                    return ("cv%d" % cc, "cv", lambda tok, cc=cc: c_conv(hbuf, cc), [])

                chains = [("latq", None, lambda tok: c_lat(hbuf, 0), []), ("latkv", None, lambda tok: c_lat(hbuf, 1), []),
                          ("kpe", "hp", lambda tok: c_kpe(hbuf, tcols, tok), []), CV(0)]
                if t + 1 < NBA:
                    chains.append(("normx", "sqn", lambda tok: c_normx(t + 1), []))
                chains += [CV(1), QH(0), KP(0), QH(1), CV(2), KP(1), QH(2), QH(3), CV(3), KP(2), QH(4), KP(3),
                           QH(5), ("v", None, lambda tok: c_v(t), ["latkv"]), QH(6),
                           ("convnorm", "sqn", lambda tok: c_convnorm(t, tsl), ["cv0", "cv1", "cv2", "cv3"]), QH(7)]
                run_chains(chains, WCH, {"hp": list(range(K_HP)), "ks": list(range(K_KS)), "cv": [0], "sqn": [0]})
                run_chains([c_kfin(t, tsl), c_qfin(t, tsl)], 2)
                if l == 0 and t == 0:
                    emit_casts(0, "C")

            A.reset(const_mark)
            kt_s = A.alloc(BF16, [8, S], parts=96)
            vv_s = A.alloc(BF16, [32, 8 * 65])
            qblk2 = [A.alloc(BF16, [8, TB], parts=96) for _ in range(2)]
            ptb = [A.alloc(BF16, [TB]) for _ in range(K_LA + 2)]
            osb = A.alloc(F32, [TB])
            rden = A.alloc(F32, [TB])
            attn_u = A.alloc(F32, [4, TB])
            attn_n = A.alloc(BF16, [4, TB])
            sqB = [A.alloc(BF16, [TB]) for _ in range(4)]
            tmpB = A.alloc(F32, [TB])
            rstdB = A.alloc(F32, [TB])
            vv4 = vv_s.ap.rearrange("p a (h x) -> p a h x", x=65)

            def loadq(j):
                qb = qblk2[j % 2]
                P.dma("sp", mk("dma_start", out=qb.ap, in_=QTD[:, :, j * TB:(j + 1) * TB]), reads=[dk("QT", i) for i in range(j * RAB, (j + 1) * RAB)], writes=[qb])

            def loadkv(j):
                P.dma("sp", mk("dma_start", out=kt_s.ap[:, :, j * TB:(j + 1) * TB], in_=KTD[:, :, j * TB:(j + 1) * TB]),
                      reads=[dk("KT", i) for i in range(j * RAB, (j + 1) * RAB)], writes=[kt_s] if j == 0 else [[("kt", j)]])
                P.dma("sp", mk("dma_start", out=vv_s.ap[:, 4 * j:4 * j + 4], in_=VVD[:, 4 * j:4 * j + 4, :]),
                      reads=[dk("VV", i) for i in range(j * RAB, (j + 1) * RAB)], writes=[vv_s] if j == 0 else [[("vv", j)]])

            loadq(0)
            loadkv(0)
            if l + 1 < nl:
                defer_casts[0] = True
                emit_casts(l + 1, "A")
                emit_casts(l + 1, "C")
                defer_casts[0] = False
            SCALE = 96.0 ** -0.5

            def tail_chain(h, pso, pko, j=0):
                P.c("dve", mk("tensor_copy", out=osb.ap[0:65], in_=pso[0:65, :]), reads=[pko], writes=[osb])
                release(pko)
                if j >= 2:
                    P.c("dve", mk("reciprocal", out=rden.ap[64:65], in_=osb.ap[64:65]), reads=[osb], writes=[rden])
                    yield
                else:
                    P.c("act", mk("activation", out=rden.ap[64:65], in_=osb.ap[64:65], func=AF.Ln), reads=[osb], writes=[rden])
                    P.c("act", mk("activation", out=rden.ap[64:65], in_=rden.ap[64:65], func=AF.Exp, scale=-1.0), reads=[rden], writes=[rden])
                yield
                psb_, pkb_ = nextps()
                P.c("pe", mk("matmul", psb_[0:64, :], lhsT=ONESF[64:65, 0:64], rhs=rden.ap[64:65], start=True, stop=True),
                    reads=[rden, cm_f], writes=[pkb_])
                plo = (h % 2) * 64
                P.c("dve", mk("tensor_tensor", out=attn_u.ap[plo:plo + 64, h // 2], in0=osb.ap[0:64], in1=psb_[0:64, :], op=ALU.mult),
                    reads=[osb, pkb_], writes=[attn_u.sub(h // 2)])

            def block_tail(j):
                for c in range(4):
                    P.c("pool", mk("tensor_tensor", out=sqB[c].ap, in0=attn_u.ap[:, c], in1=attn_u.ap[:, c], op=ALU.mult),
                        reads=[attn_u.sub(c)], writes=[sqB[c]])
                yield
                yield
                ps, pk = nextps()
                for c in range(4):
                    P.c("pe", mk("matmul", ps[:, :], lhsT=ONES, rhs=sqB[c].ap, start=(c == 0), stop=(c == 3)),
                        reads=[sqB[c], cm_b], writes=[pk])
                rms_rstd(ps, pk, 1.0 / 512, 128, tmpB, rstdB)
                yield
                for c in range(4):
                    P.c("dve", mk("scalar_tensor_tensor",
                        out=attn_n.ap[:, c], in0=attn_u.ap[:, c], scalar=G(G_OA, l, c), in1=rstdB.ap,
                        op0=ALU.mult, op1=ALU.mult),
                        reads=[attn_u.sub(c), rstdB, gp], writes=[attn_n.sub(c)])
                P.dma("sp", mk("dma_start", out=MIXA[:, :, j * TB:(j + 1) * TB], in_=attn_n.ap), reads=[attn_n], writes=[dk("MIXA", j)])

            for j in range(NB):
                if j + 1 < NB:
                    loadq(j + 1)
                    loadkv(j + 1)
                nkc = 4 * j + 4
                qb = qblk2[j % 2]
                tasks = [(h, kc) for h in range(8) for kc in range(nkc)]
                stiles = {}
                psos = {}
                LA = K_LA

                def issue_s(i):
                    h, kc = tasks[i]
                    r = kc - 4 * j
                    qoff = 128 * r if r > 0 else 0
                    n = TB - qoff
                    pss, pks = nextps(hold=True)
                    kdeps = [kt_s] if kc < 4 else [[("kt", kc // 4)]]
                    P.c("pe", mk("matmul",
                        pss[:, 0:n], lhsT=kt_s.ap[:, h, kc * 128:(kc + 1) * 128], rhs=qb.ap[0:96, h, qoff:TB], start=True, stop=True),
                        reads=kdeps + [qb.sub(h)], writes=[pks])
                    stiles[i] = (pss, pks, r, qoff, n)

                for i in range(min(LA, len(tasks))):
                    issue_s(i)
                for i, (h, kc) in enumerate(tasks):
                    if kc == 0:
                        if j >= 1:
                            flush_casts(1 if j < NB - 1 else 1000)
                        while len(held) - len(stiles) > 1:
                            step()
                        psos[h] = nextps(hold=True)
                    pso, pko = psos[h]
                    pss, pks, r, qoff, n = stiles.pop(i)
                    pt = ptb[i % (K_LA + 2)]
                    vdeps = [vv_s] if kc < 4 else [[("vv", kc // 4)]]
                    P.c("act", mk("activation", out=pt.ap[:, 0:n], in_=pss[:, 0:n], func=AF.Exp, scale=SCALE),
                        reads=[pks], writes=[pt])
                    release(pks)
                    if r >= 0:
                        P.c("pool", mk("tensor_tensor", out=pt.ap[:, 0:128], in0=pt.ap[:, 0:128], in1=TRI, op=ALU.mult),
                            reads=[pt, cm_b], writes=[pt])
                    if i + LA < len(tasks):
                        issue_s(i + LA)
                    P.c("pe", mk("matmul",
                        pso[0:65, qoff:TB], lhsT=vv4[:, kc, h, :], rhs=pt.ap[:, 0:n], start=(kc == 0), stop=(kc == nkc - 1)),
                        reads=vdeps + [pt], writes=[pko])
                    if i % K_STEP == K_STEP - 1:
                        step()
                    if kc == nkc - 1:
                        if h == 7:
                            drain()
                        bg.append(tail_chain(h, pso, pko, j))
                        if h == 7:
                            bg.append(block_tail(j))
            drain()

            A.reset(const_mark)
            NRING = 6
            ring = [A.alloc(BF16, [PIECE]) for _ in range(NRING)]
            xc2 = [A.alloc(F32, [8, TB]) for _ in range(2)]
            hC = A.alloc(BF16, [8, TB])
            sqC = [A.alloc(BF16, [TB]) for _ in range(3)]
            uC = A.alloc(BF16, [32, TB])
            mixa2 = [A.alloc(BF16, [4, TB]) for _ in range(2)]
            mixc2 = [A.alloc(BF16, [4, TB]) for _ in range(2)]
            pb2 = [A.alloc(BF16, [2, TB]) for _ in range(2)]
            gate = A.alloc(F32, [TB])
            gtmp = A.alloc(F32, [TB])
            relu2 = [A.alloc(F32, [TB]) for _ in range(2)]
            tmpC = A.alloc(F32, [TB])
            rstdC = A.alloc(F32, [TB])
            ringctr = [0]

            def loadpiece(i):
                slot = ring[ringctr[0] % NRING]
                ringctr[0] += 1
                if i == 20:
                    P.dma("sp", mk("dma_start", out=slot.ap[:, 0:2048], in_=WC_B[l, i][:, 0:2048]), reads=[dk("WC", l, i)], writes=[slot])
                else:
                    P.dma("sp", mk("dma_start", out=slot.ap, in_=WC_B[l, i]), reads=[dk("WC", l, i)], writes=[slot])
                return slot

            def loadblk(t):
                xb = xc2[t % 2]
                P.dma("sp", mk("dma_start", out=xb.ap, in_=xview(xsrc(l), t)), reads=[dk("X", t)], writes=[xb])
                ma = mixa2[t % 2]
                P.dma("sp", mk("dma_start", out=ma.ap, in_=MIXA[:, :, t * TB:(t + 1) * TB]), reads=[dk("MIXA", t)], writes=[ma])
                mc = mixc2[t % 2]
                P.dma("sp", mk("dma_start", out=mc.ap, in_=MIXC[:, :, t * TB:(t + 1) * TB]), reads=[dk("MIXC", i) for i in range(t * RAB, (t + 1) * RAB)], writes=[mc])
                pb = pb2[t % 2]
                P.dma("pool", mk("dma_start", out=pb.ap, in_=pT[l].rearrange("(kc p) t -> p kc t", p=128)[:, :, t * TB:(t + 1) * TB]),
                      writes=[pb])

            sched = [(t, i) for t in range(NB) for i in range(NPIECE)]
            loaded = {}
            nload = [0]

            def ensure(upto):
                while nload[0] < len(sched) and nload[0] <= upto:
                    tt, ii = sched[nload[0]]
                    loaded[(tt, ii)] = loadpiece(ii)
                    nload[0] += 1

            loadblk(0)
            ensure(NRING - 1)
            for t in range(NB):
                if t + 1 < NB:
                    loadblk(t + 1)
                xb = xc2[t % 2]
                ma = mixa2[t % 2]
                mc = mixc2[t % 2]
                pb = pb2[t % 2]
                base = t * NPIECE

                def piece(i, t=t, base=base):
                    ensure(base + i)
                    return loaded[(t, i)]

                def done(i, base=base):
                    ensure(base + i + NRING)

                po0, po1 = piece(0), piece(1)
                wo0 = po0.ap.rearrange("p (c n) -> p c n", n=1024)
                wo1 = po1.ap.rearrange("p (c n) -> p c n", n=1024)
                for oc in range(8):
                    ps, pk = nextps()
                    ocs = slice(oc * 128, (oc + 1) * 128)
                    for c in range(4):
                        P.c("pe", mk("matmul", ps[:, :], lhsT=wo0[:, c, ocs], rhs=ma.ap[:, c], start=(c == 0), stop=False),
                            reads=[po0, ma.sub(c)], writes=[pk])
                    for c in range(4):
                        P.c("pe", mk("matmul", ps[:, :], lhsT=wo1[:, c, ocs], rhs=mc.ap[:, c], start=False, stop=(c == 3)),
                            reads=[po1, mc.sub(c)], writes=[pk])
                    P.c("dve", mk("tensor_tensor", out=xb.ap[:, oc], in0=xb.ap[:, oc], in1=ps[:, :], op=ALU.add),
                        reads=[xb.sub(oc), pk], writes=[xb.sub(oc)])
                done(0)
                done(1)
                norm_block(xb, 8, G_MLP, l, 1.0 / D, sqC, hC, tmpC, rstdC)
                for g in range(8):
                    pc = piece(2 + g)
                    wup = pc.ap.rearrange("p (kc n) -> p kc n", n=512)
                    for f in range(4):
                        ps, pk = nextps()
                        for kc in range(8):
                            P.c("pe", mk("matmul", ps[:, :], lhsT=wup[:, kc, f * 128:(f + 1) * 128], rhs=hC.ap[:, kc],
                                                                                   start=(kc == 0), stop=(kc == 7)),
                                reads=[pc, hC.sub(kc)], writes=[pk])
                        rl = relu2[(4 * g + f) % 2]
                        P.c("act", mk("activation", out=rl.ap, in_=ps[:, :], func=AF.Relu), reads=[pk], writes=[rl])
                        P.c("dve", mk("tensor_tensor", out=uC.ap[:, 4 * g + f], in0=rl.ap, in1=rl.ap, op=ALU.mult),
                            reads=[rl], writes=[uC.sub(4 * g + f)])
                    done(2 + g)
                for oc in range(8):
                    pc = piece(10 + oc)
                    wdn = pc.ap.rearrange("p (fc n) -> p fc n", n=128)
                    ps, pk = nextps()
                    for fc in range(32):
                        P.c("pe", mk("matmul", ps[:, :], lhsT=wdn[:, fc, :], rhs=uC.ap[:, fc], start=(fc == 0), stop=(fc == 31)),
                            reads=[pc, uC.sub(fc)], writes=[pk])
                    P.c("dve", mk("tensor_tensor", out=xb.ap[:, oc], in0=xb.ap[:, oc], in1=ps[:, :], op=ALU.add),
                        reads=[xb.sub(oc), pk], writes=[xb.sub(oc)])
                    done(10 + oc)
                norm_block(xb, 8, G_PLE, l, 1.0 / D, sqC, hC, tmpC, rstdC)
                pg = [piece(18), piece(19)]
                pp = piece(20)
                wpl = pp.ap[:, 0:2048].rearrange("p (kc n) -> p kc n", n=1024)
                for oc in range(8):
                    pgi = pg[oc // 4]
                    wgv = pgi.ap.rearrange("p (kc n) -> p kc n", n=512)
                    o4 = (oc % 4) * 128
                    ps, pk = nextps()
                    for kc in range(8):
                        P.c("pe", mk("matmul", ps[:, :], lhsT=wgv[:, kc, o4:o4 + 128], rhs=hC.ap[:, kc],
                                                                                 start=(kc == 0), stop=(kc == 7)),
                            reads=[pgi, hC.sub(kc)], writes=[pk])
                    P.c("act", mk("activation", out=gate.ap, in_=ps[:, :], func=AF.Sigmoid), reads=[pk], writes=[gate])
                    ps2, pk2 = nextps()
                    for kc in range(2):
                        P.c("pe", mk("matmul", ps2[:, :], lhsT=wpl[:, kc, oc * 128:(oc + 1) * 128], rhs=pb.ap[:, kc],
                                                                         start=(kc == 0), stop=(kc == 1)),
                            reads=[pp, pb.sub(kc)], writes=[pk2])
                    P.c("dve", mk("tensor_tensor", out=gtmp.ap, in0=gate.ap, in1=ps2[:, :], op=ALU.mult),
                        reads=[gate, pk2], writes=[gtmp])
                    P.c("pool", mk("tensor_tensor", out=xb.ap[:, oc], in0=xb.ap[:, oc], in1=gtmp.ap, op=ALU.add),
                        reads=[xb.sub(oc), gtmp], writes=[xb.sub(oc)])
                done(18)
                done(19)
                done(20)
                so = P.dma("sp", mk("dma_start", out=xview(outT, t), in_=xb.ap), reads=[xb], writes=[dk("X", t)])
                if l == nl - 1:
                    out_stores.append(so)

        P.emit(final_ops=out_stores)
        build.stats = P.stats
    return nc


def _gpack(inp):
    g = np.zeros((128, NG), np.float32)
    for l in range(NL):
        for kind, name in ((G_MIX, "g_mix"), (G_MLP, "g_mlp"), (G_PLE, "g_ple")):
            for c in range(8):
                g[:, gcol(kind, l, c)] = inp[name][l, c * 128:(c + 1) * 128]
        for c in range(3):
            g[:, gcol(G_QL, l, c)] = inp["g_q_lat"][l, c * 128:(c + 1) * 128]
        for c in range(2):
            g[:, gcol(G_KVL, l, c)] = inp["g_kv_lat"][l, c * 128:(c + 1) * 128]
        for c in range(4):
            g[:, gcol(G_OC, l, c)] = inp["g_out_conv"][l, c * 128:(c + 1) * 128]
        for j in range(3):
            for c in range(4):
                g[:, gcol(G_CW, l, j * 4 + c)] = inp["conv_w"][l, j, c * 128:(c + 1) * 128]
        for c in range(4):
            g[:, gcol(G_OA, l, c)] = inp["g_out_attn"][l, c * 128:(c + 1) * 128]
        g[0:64, gcol(G_QN, l)] = inp["g_qn_nope"][l]
        g[64:96, gcol(G_QN, l)] = inp["g_qn_rope"][l]
        g[0:64, gcol(G_KN2, l)] = inp["g_kn_nope"][l]
        g[64:128, gcol(G_KN2, l)] = inp["g_kn_nope"][l]
        g[64:96, gcol(G_KR, l)] = inp["g_kn_rope"][l]
    return g


def _cmat():
    c = np.zeros((128, NCM), np.float32)
    c[:, C_ONES:C_ONES + 128] = 1.0
    c[0:64, C_BLK96:C_BLK96 + 64] = 1.0 / 64
    c[64:96, C_BLK96 + 64:C_BLK96 + 96] = 1.0 / 32
    c[0:64, C_BLK2:C_BLK2 + 64] = 1.0 / 64
    c[64:128, C_BLK2 + 64:C_BLK2 + 128] = 1.0 / 64
    for i in range(16):
        c[80 + i, C_ROT + 64 + i] = -1.0
        c[64 + i, C_ROT + 80 + i] = 1.0
    kq = np.arange(128)
    c[:, C_TRI:C_TRI + 128] = (kq[None, :] >= kq[:, None]).astype(np.float32)
    f = (1.0 / (10000.0 ** (np.arange(0, 32, 2, dtype=np.float32) / 32))).astype(np.float32)
    c[64:80, C_INVF] = f
    c[80:96, C_INVF] = f
    return c


_NC_CACHE = {}


def kernel(**inp):
    inp = {k: np.asarray(v) for k, v in inp.items()}
    if "nc" not in _NC_CACHE:
        _NC_CACHE["nc"] = build()
    nc = _NC_CACHE["nc"]
    gp = _gpack(inp)
    cm = _cmat()
    shared = {k: np.ascontiguousarray(inp[k], dtype=np.float32) for k in
              ("w_in", "w_uq", "w_ukv", "w_o", "w_up", "w_down", "w_ple_gate", "w_ple")}
    in_maps = []
    for b in range(8):
        m = dict(shared)
        m["xT"] = np.ascontiguousarray(inp["x"][b].T)
        m["pT"] = np.ascontiguousarray(np.transpose(inp["p"][:, b], (0, 2, 1)))
        m["pos"] = np.ascontiguousarray(inp["positions"][b][None, :].astype(np.int32))
        m["gpack"] = gp
        m["cmat"] = cm
        in_maps.append(m)
    res = run_bass_kernel_spmd(nc, in_maps, core_ids=list(range(8)))
    out = np.stack([np.ascontiguousarray(r["outT"].T) for r in res.results], axis=0)
    return out.astype(np.float32)
```

```python
import contextlib
import numpy as np
import concourse.bass as bass
import concourse.mybir as mybir
from concourse.bass_utils import run_bass_kernel_spmd

F32 = mybir.dt.float32
BF16 = mybir.dt.bfloat16
I32 = mybir.dt.int32
ALU = mybir.AluOpType
AF = mybir.ActivationFunctionType

S = 4096
D = 1024
NL = 4
TB = 512
NB = S // TB
TA = 512
NBA = S // TA
WCH = 6
K_HP = 2
K_KS = 2
K_LA = 4
K_STEP = 4
RAB = TB // TA
EPS = 1e-6
INC = 2208
NPIECE = 21
PIECE = 4096

G_MIX, G_MLP, G_PLE, G_QL, G_KVL, G_OC, G_CW, G_OA, G_QN, G_KN2, G_KR = range(11)
_GW = [8, 8, 8, 3, 2, 4, 12, 8, 1, 1, 1]
_GOFF = np.concatenate([[0], np.cumsum([w * NL for w in _GW])]).astype(int)
NG = int(_GOFF[-1])


def gcol(kind, l, i=0):
    return int(_GOFF[kind] + l * _GW[kind] + i)


C_ONES, C_BLK96, C_BLK2, C_ROT, C_TRI = 0, 128, 256, 384, 512
C_INVF = 640
NCM = 641


class Op:
    __slots__ = ("eng", "fn", "kind", "deps", "signal", "sem", "sigval", "waits", "idx")

    def __init__(self, eng, fn, kind):
        self.eng = eng
        self.fn = fn
        self.kind = kind
        self.deps = []
        self.signal = False
        self.sem = None
        self.sigval = 0
        self.waits = []


SAME_ENG_FULL = ("pool", "dve", "act")


class Prog:
    ENGS = ("pe", "act", "dve", "pool", "sp")
    NSLOT = {"pe": 2, "act": 2, "dve": 2, "pool": 64, "sp": 24}

    def __init__(self, nc):
        self.nc = nc
        self.ops = []
        self.last_writer = {}
        self.readers = {}

    def _add(self, eng, fn, kind, reads, writes):
        op = Op(eng, fn, kind)
        op.idx = len(self.ops)
        deps = {}
        rk = []
        for r in reads:
            rk.extend(r)
        wk = []
        for w in writes:
            wk.extend(w)
        for k in rk:
            w = self.last_writer.get(k)
            if w is not None:
                deps[w.idx] = (w, True)
        for k in wk:
            w = self.last_writer.get(k)
            if w is not None and w.idx not in deps:
                deps[w.idx] = (w, False)
            for r in self.readers.get(k, ()):
                if r.idx not in deps:
                    deps[r.idx] = (r, False)
        for (d, raw) in deps.values():
            if d.kind == "c" and kind == "c" and d.eng == eng:
                if eng == "pe" or (not raw and eng not in SAME_ENG_FULL):
                    continue
            op.deps.append(d)
            d.signal = True
        for k in wk:
            self.last_writer[k] = op
            self.readers[k] = []
        for k in rk:
            self.readers.setdefault(k, []).append(op)
        self.ops.append(op)
        return op

    def c(self, eng, fn, reads=(), writes=()):
        return self._add(eng, fn, "c", reads, writes)

    def dma(self, eng, fn, reads=(), writes=()):
        op = self._add(eng, fn, "d", reads, writes)
        op.signal = True
        return op

    def emit(self, final_ops=(), final_eng="sp"):
        nc = self.nc
        with contextlib.ExitStack() as st:
            csem = {e: st.enter_context(nc.semaphore("c_" + e)) for e in self.ENGS}
            dsem = {e: [st.enter_context(nc.semaphore("d_%s_%d" % (e, i))) for i in range(self.NSLOT[e])]
                    for e in self.ENGS}
            ccount = {e: 0 for e in self.ENGS}
            dcount = {e: 0 for e in self.ENGS}
            slotlast = {e: [None] * self.NSLOT[e] for e in self.ENGS}
            slotcnt = {e: [0] * self.NSLOT[e] for e in self.ENGS}
            waited = {e: {} for e in self.ENGS}
            for op in self.ops:
                w = waited[op.eng]
                waits = []

                def need(sem, val):
                    if w.get(id(sem), 0) < val:
                        w[id(sem)] = val
                        waits.append((sem, val))

                for d in op.deps:
                    need(d.sem, d.sigval)
                if op.kind == "d":
                    e = op.eng
                    s = dcount[e] % self.NSLOT[e]
                    dcount[e] += 1
                    prev = slotlast[e][s]
                    if prev is not None:
                        need(prev.sem, prev.sigval)
                    slotcnt[e][s] += 1
                    op.sem = dsem[e][s]
                    op.sigval = 16 * slotcnt[e][s]
                    slotlast[e][s] = op
                elif op.signal:
                    ccount[op.eng] += 1
                    op.sem = csem[op.eng]
                    op.sigval = ccount[op.eng]
                op.waits = waits
            fin = []
            wf = waited[final_eng]
            for d in final_ops:
                if wf.get(id(d.sem), 0) < d.sigval:
                    wf[id(d.sem)] = d.sigval
                    fin.append((d.sem, d.sigval))
            per = {e: [o for o in self.ops if o.eng == e] for e in self.ENGS}
            self.stats = {e: len(per[e]) for e in self.ENGS}
            self.stats["sig"] = dict(ccount)

            def run(eng_obj, e):
                for op in per[e]:
                    for (sem, val) in op.waits:
                        eng_obj.wait_ge(sem, val)
                    ins = op.fn(eng_obj)
                    if op.signal:
                        ins.then_inc(op.sem, 16 if op.kind == "d" else 1)
                if e == final_eng:
                    for (sem, val) in fin:
                        eng_obj.wait_ge(sem, val)

            with nc.Block() as block:
                @block.tensor
                def _(eng):
                    run(eng, "pe")

                @block.scalar
                def _(eng):
                    run(eng, "act")

                @block.vector
                def _(eng):
                    run(eng, "dve")

                @block.gpsimd
                def _(eng):
                    run(eng, "pool")

                @block.sync
                def _(eng):
                    run(eng, "sp")


PAGE = 256


def mk(method, *args, **kw):
    return lambda e: getattr(e, method)(*args, **kw)


class Reg:
    def __init__(self, ap, lo, hi):
        self.ap = ap
        self.lo = lo
        self.hi = hi
        self.keys = list(range(lo // PAGE, (hi - 1) // PAGE + 1))

    def __iter__(self):
        return iter(self.keys)

    def sub(self, i, n=1):
        nfirst = self.ap.shape[1]
        step = (self.hi - self.lo) // nfirst
        ap = self.ap[:, i] if n == 1 else self.ap[:, i:i + n]
        return Reg(ap, self.lo + i * step, self.lo + (i + n) * step)


class Arena:
    def __init__(self, tile, nbytes):
        self.t = tile
        self.n = nbytes
        self.off = 0
        self.marks = []

    def alloc(self, dtype, free, parts=128):
        es = 4 if dtype in (F32, I32) else 2
        n = es
        for f in free:
            n *= f
        lo = (self.off + 255) // 256 * 256
        hi = lo + n
        assert hi <= self.n, "arena overflow %d > %d" % (hi, self.n)
        self.off = hi
        ap = self.t[0:parts, lo // 2:hi // 2]
        if dtype != BF16:
            ap = ap.bitcast(dtype)
        if len(free) == 2:
            ap = ap.rearrange("p (a b) -> p a b", b=free[1])
        elif len(free) == 3:
            ap = ap.rearrange("p (a b c) -> p a b c", b=free[1], c=free[2])
        return Reg(ap, lo, hi)

    def mark(self):
        return self.off

    def reset(self, m):
        self.off = m


def build(nl=NL, dbg=False):
    nc = bass.Bass("TRN2", target_bir_lowering=False)
    dt_in = lambda name, shape, dt=F32: nc.dram_tensor(name, shape, dt, kind="ExternalInput").ap()
    xT = dt_in("xT", [D, S])
    pT = dt_in("pT", [NL, 256, S])
    pos = dt_in("pos", [1, S], I32)
    w_in = dt_in("w_in", [NL, D, INC])
    w_uq = dt_in("w_uq", [NL, 384, 768])
    w_ukv = dt_in("w_ukv", [NL, 256, 1024])
    w_o = dt_in("w_o", [NL, D, D])
    w_up = dt_in("w_up", [NL, D, 4096])
    w_down = dt_in("w_down", [NL, 4096, D])
    w_g = dt_in("w_ple_gate", [NL, D, D])
    w_ple = dt_in("w_ple", [NL, 256, D])
    gpack = dt_in("gpack", [128, NG])
    cmat = dt_in("cmat", [128, NCM])
    outT = nc.dram_tensor("outT", [D, S], F32, kind="ExternalOutput").ap()

    WIN_B = nc.dram_tensor("WIN_B", [NL, 128, 8 * INC], BF16).ap()
    WK_B = nc.dram_tensor("WK_B", [NL, 128, 1024], BF16).ap()
    WV_B = nc.dram_tensor("WV_B", [NL, 128, 1024], BF16).ap()
    WUQ_B = nc.dram_tensor("WUQ_B", [NL, 128, 3 * 768], BF16).ap()
    WC_B = nc.dram_tensor("WC_B", [NL, NPIECE, 128, PIECE], BF16).ap()
    sk = "ExternalOutput" if dbg else "Internal"
    QTD = nc.dram_tensor("QTD", [96, 8, S], BF16, kind=sk).ap()
    KTD = nc.dram_tensor("KTD", [96, 8, S], BF16, kind=sk).ap()
    VVD = nc.dram_tensor("VVD", [128, 32, 8 * 65], BF16, kind=sk).ap()
    MIXA = nc.dram_tensor("MIXA", [128, 4, S], BF16, kind=sk).ap()
    MIXC = nc.dram_tensor("MIXC", [128, 4, S], BF16, kind=sk).ap()
    COSD = nc.dram_tensor("COSD", [96, S], F32).ap()
    SIND = nc.dram_tensor("SIND", [96, S], F32).ap()

    ARENA_BYTES = 204 * 1024
    with contextlib.ExitStack() as st:
        arena_t = st.enter_context(nc.sbuf_tensor("arena", [128, ARENA_BYTES // 2], BF16))
        pst = [st.enter_context(nc.psum_tensor("ps%d" % i, [128, 512], F32)) for i in range(8)]
        A = Arena(arena_t, ARENA_BYTES)
        P = Prog(nc)
        psk = [[("ps", i)] for i in range(8)]
        psctr = [0]

        held = set()

        def nextps(hold=False):
            while True:
                i = psctr[0] % 8
                psctr[0] += 1
                if i not in held:
                    break
            if hold:
                held.add(i)
            return pst[i], psk[i]

        def release(pk):
            held.discard(pk[0][1])

        def dk(name, *idx):
            return [("d", name) + tuple(idx)]

        gp = A.alloc(F32, [NG])
        cm_f = A.alloc(F32, [NCM])
        cm_b = A.alloc(BF16, [640])
        epsc = A.alloc(F32, [1])
        P.dma("sp", mk("dma_start", out=gp.ap, in_=gpack), writes=[gp])
        P.dma("sp", mk("dma_start", out=cm_f.ap, in_=cmat), writes=[cm_f])
        P.c("dve", mk("tensor_copy", out=cm_b.ap, in_=cm_f.ap[:, 0:640]), reads=[cm_f], writes=[cm_b])
        P.c("dve", mk("memset", epsc.ap, EPS), writes=[epsc])
        ONES = cm_b.ap[:, C_ONES:C_ONES + 128]
        BLK96 = cm_b.ap[0:96, C_BLK96:C_BLK96 + 96]
        BLK2 = cm_b.ap[:, C_BLK2:C_BLK2 + 128]
        ROT = cm_b.ap[0:96, C_ROT:C_ROT + 96]
        TRI = cm_b.ap[:, C_TRI:C_TRI + 128]
        ONESF = cm_f.ap[:, C_ONES:C_ONES + 64]

        def G(kind, l, i=0, lo=0, hi=128):
            c = gcol(kind, l, i)
            return gp.ap[lo:hi, c:c + 1]

        cast_q = []
        defer_casts = [False]

        def pdma_cast(fn, writes):
            if defer_casts[0]:
                cast_q.append((fn, writes))
            else:
                P.dma("pool", fn, writes=writes)

        def flush_casts(n):
            for _ in range(min(n, len(cast_q))):
                fn, writes = cast_q.pop(0)
                P.dma("pool", fn, writes=writes)

        def emit_casts(l, part):
            if part == "A":
                emit_casts_a(l)
            else:
                emit_casts_c(l)

        def emit_casts_a(l):
            v = w_in[l].rearrange("(kc p) n -> p kc n", p=128)
            dv = WIN_B[l].rearrange("p (kc n) -> p kc n", n=INC)
            for kc in range(8):
                pdma_cast(mk("dma_start", out=dv[:, kc], in_=v[:, kc]), [dk("WIN", l, kc)])
            v = w_ukv[l].rearrange("(kc p) (h x) -> p kc h x", p=128, x=128)
            for kc in range(2):
                pdma_cast(mk("dma_start",
                    out=WK_B[l].rearrange("p (kc h x) -> p kc h x", kc=2, x=64)[:, kc], in_=v[:, kc, :, 0:64]), [dk("WK", l, kc)])
                pdma_cast(mk("dma_start",
                    out=WV_B[l].rearrange("p (kc h x) -> p kc h x", kc=2, x=64)[:, kc], in_=v[:, kc, :, 64:128]), [dk("WV", l, kc)])
            pdma_cast(mk("dma_start",
                out=WUQ_B[l].rearrange("p (kc n) -> p kc n", n=768),
                in_=w_uq[l].rearrange("(kc p) n -> p kc n", p=128)), [dk("WUQ", l)])

        def emit_casts_c(l):
            for i in range(2):
                pdma_cast(mk("dma_start",
                    out=WC_B[l, i].rearrange("p (c n) -> p c n", n=1024),
                    in_=w_o[l][512 * i:512 * (i + 1)].rearrange("(c p) n -> p c n", p=128)), [dk("WC", l, i)])
            vu = w_up[l].rearrange("(kc p) n -> p kc n", p=128)
            for g in range(8):
                pdma_cast(mk("dma_start",
                    out=WC_B[l, 2 + g].rearrange("p (kc n) -> p kc n", n=512), in_=vu[:, :, g * 512:(g + 1) * 512]), [dk("WC", l, 2 + g)])
            vd = w_down[l].rearrange("(fc p) n -> p fc n", p=128)
            for oc in range(8):
                pdma_cast(mk("dma_start",
                    out=WC_B[l, 10 + oc].rearrange("p (fc n) -> p fc n", n=128), in_=vd[:, :, oc * 128:(oc + 1) * 128]), [dk("WC", l, 10 + oc)])
            vg = w_g[l].rearrange("(kc p) n -> p kc n", p=128)
            for i in range(2):
                pdma_cast(mk("dma_start",
                    out=WC_B[l, 18 + i].rearrange("p (kc n) -> p kc n", n=512), in_=vg[:, :, i * 512:(i + 1) * 512]), [dk("WC", l, 18 + i)])
            pdma_cast(mk("dma_start",
                out=WC_B[l, 20][:, 0:2048].rearrange("p (kc n) -> p kc n", n=1024),
                in_=w_ple[l].rearrange("(kc p) n -> p kc n", p=128)), [dk("WC", l, 20)])

        emit_casts(0, "A")

        m0 = A.mark()
        COS = A.alloc(F32, [S], parts=96)
        SIN = A.alloc(F32, [S], parts=96)
        posi = A.alloc(I32, [S], parts=96)
        ang = A.alloc(F32, [S], parts=96)
        kk = A.alloc(F32, [S], parts=96)
        ki = A.alloc(I32, [S], parts=96)
        P.dma("sp", mk("dma_start", out=posi.ap, in_=pos.partition_broadcast(96)), writes=[posi])
        P.c("dve", mk("tensor_copy", out=ang.ap, in_=posi.ap), reads=[posi], writes=[ang])
        INVF = cm_f.ap[0:96, C_INVF:C_INVF + 1]
        P.c("dve", mk("tensor_scalar", out=ang.ap, in0=ang.ap, scalar1=INVF, scalar2=None, op0=ALU.mult),
            reads=[ang, cm_f], writes=[ang])
        TWO_PI = 2.0 * np.pi
        C1 = 6.28125
        C2 = TWO_PI - C1
        for (tab, shift) in ((SIN, 0.0), (COS, np.pi / 2)):
            P.c("dve", mk("tensor_scalar",
                out=tab.ap, in0=ang.ap, scalar1=float(shift), scalar2=None, op0=ALU.add), reads=[ang], writes=[tab])
            P.c("dve", mk("tensor_scalar",
                out=ki.ap, in0=tab.ap, scalar1=1.0 / TWO_PI, scalar2=None, op0=ALU.mult), reads=[tab], writes=[ki])
            P.c("dve", mk("tensor_copy", out=kk.ap, in_=ki.ap), reads=[ki], writes=[kk])
            P.c("dve", mk("scalar_tensor_tensor",
                out=tab.ap, in0=kk.ap, scalar=-C1, in1=tab.ap, op0=ALU.mult, op1=ALU.add), reads=[kk, tab], writes=[tab])
            P.c("dve", mk("scalar_tensor_tensor",
                out=tab.ap, in0=kk.ap, scalar=-C2, in1=tab.ap, op0=ALU.mult, op1=ALU.add), reads=[kk, tab], writes=[tab])
            P.c("dve", mk("tensor_scalar",
                out=tab.ap, in0=tab.ap, scalar1=-np.pi, scalar2=np.pi, op0=ALU.max, op1=ALU.min), reads=[tab], writes=[tab])
            P.c("act", mk("activation", out=tab.ap, in_=tab.ap, func=AF.Sin), reads=[tab], writes=[tab])
        P.dma("sp", mk("dma_start", out=COSD, in_=COS.ap), reads=[COS], writes=[dk("COSD")])
        P.dma("sp", mk("dma_start", out=SIND, in_=SIN.ap), reads=[SIN], writes=[dk("SIND")])
        A.reset(m0)
        const_mark = A.mark()

        def rms_rstd(ssq_ps, ssq_k, scale, parts, tmp, rstd, n=TB):
            P.c("act", mk("activation", out=tmp.ap[0:parts, 0:n], in_=ssq_ps[0:parts, 0:n], func=AF.Ln,
                                              bias=epsc.ap[0:parts], scale=float(scale)),
                reads=[ssq_k, epsc], writes=[tmp])
            P.c("act", mk("activation", out=rstd.ap[0:parts, 0:n], in_=tmp.ap[0:parts, 0:n], func=AF.Exp, scale=-0.5),
                reads=[tmp], writes=[rstd])

        def norm_block(xb, nch, gkind, l, scale, sqb, hb, tmp, rstd):
            ps, pk = nextps()
            for c in range(nch):
                sq = sqb[c % len(sqb)]
                P.c("act", mk("activation", out=sq.ap, in_=xb.ap[:, c], func=AF.Square),
                    reads=[xb.sub(c)], writes=[sq])
                P.c("pe", mk("matmul", ps[:, :], lhsT=ONES, rhs=sq.ap, start=(c == 0), stop=(c == nch - 1)),
                    reads=[sq, cm_b], writes=[pk])
            rms_rstd(ps, pk, scale, 128, tmp, rstd)
            for c in range(nch):
                P.c("dve", mk("scalar_tensor_tensor",
                    out=hb.ap[:, c], in0=xb.ap[:, c], scalar=G(gkind, l, c), in1=rstd.ap, op0=ALU.mult, op1=ALU.mult),
                    reads=[xb.sub(c), rstd, gp], writes=[hb.sub(c)])

        def head_post_gen(raw_ps, raw_k, gcolap, cs_regs, dst_ap, dst_reg, T, held_raw=False, n=TB):
            sq, qn, t1, t2, tr = T
            cosb, sinb = cs_regs
            P.c("act", mk("activation", out=sq.ap[0:96, 0:n], in_=raw_ps[0:96, 0:n], func=AF.Square),
                reads=[raw_k], writes=[sq])
            yield
            ps2, pk2 = nextps()
            P.c("pe", mk("matmul", ps2[0:96, 0:n], lhsT=BLK96, rhs=sq.ap[0:96, 0:n], start=True, stop=True),
                reads=[sq, cm_b], writes=[pk2])
            rms_rstd(ps2, pk2, 1.0, 96, tr, tr, n)
            P.c("dve", mk("scalar_tensor_tensor", out=qn.ap[0:96, 0:n], in0=raw_ps[0:96, 0:n], scalar=gcolap,
                                                        in1=tr.ap[0:96, 0:n], op0=ALU.mult, op1=ALU.mult),
                reads=[raw_k, tr, gp], writes=[qn])
            if held_raw:
                release(raw_k)
            yield
            ps3, pk3 = nextps()
            P.c("pe", mk("matmul", ps3[0:96, 0:n], lhsT=ROT, rhs=qn.ap[0:96, 0:n], start=True, stop=True),
                reads=[qn, cm_b], writes=[pk3])
            P.c("dve", mk("tensor_tensor", out=t1.ap[0:96, 0:n], in0=qn.ap[0:96, 0:n], in1=cosb.ap[0:96, 0:n], op=ALU.mult),
                reads=[qn, cosb], writes=[t1])
            P.c("dve", mk("tensor_tensor", out=t2.ap[0:96, 0:n], in0=ps3[0:96, 0:n], in1=sinb.ap[0:96, 0:n], op=ALU.mult),
                reads=[pk3, sinb], writes=[t2])
            P.c("pool", mk("tensor_tensor", out=dst_ap, in0=t1.ap[0:96, 0:n], in1=t2.ap[0:96, 0:n], op=ALU.add),
                reads=[t1, t2], writes=[dst_reg])

        def head_post(*a):
            for _ in head_post_gen(*a):
                pass

        bg = []

        def step():
            if bg:
                try:
                    next(bg[0])
                except StopIteration:
                    bg.pop(0)

        def drain():
            while bg:
                step()

        def run_chains(chains, width, tokens=None):
            tokens = {k: list(v) for k, v in (tokens or {}).items()}
            active = []
            pending = list(chains)
            finished = set()
            while active or pending:
                i = 0
                while i < len(pending) and len(active) < width:
                    c = pending[i]
                    if isinstance(c, tuple):
                        name, cls, fac, after = c
                        ok = all(a in finished for a in after) and (cls is None or tokens[cls])
                        if not ok:
                            i += 1
                            continue
                        tok = tokens[cls].pop(0) if cls is not None else None
                        active.append((fac(tok), cls, tok, name))
                    else:
                        active.append((c, None, None, None))
                    pending.pop(i)
                assert active, "chain scheduler deadlock"
                for ent in list(active):
                    g, cls, tok, name = ent
                    try:
                        next(g)
                    except StopIteration:
                        active.remove(ent)
                        if cls is not None:
                            tokens[cls].append(tok)
                        if name is not None:
                            finished.add(name)

        def xsrc(l):
            return xT if l == 0 else outT

        def xview(ap, t):
            return ap.rearrange("(c p) t -> p c t", p=128)[:, :, t * TB:(t + 1) * TB]

        out_stores = []

        for l in range(nl):
            A.reset(const_mark)
            w_in_s = A.alloc(BF16, [8, INC])
            wk_s = A.alloc(BF16, [2, 512])
            wv_s = A.alloc(BF16, [2, 512])
            wuq_s = A.alloc(BF16, [3, 768])
            xb1 = A.alloc(F32, [8, TA])
            hb2 = [A.alloc(BF16, [8, TA]) for _ in range(2)]
            sqn = A.alloc(BF16, [8, TA])
            trn = A.alloc(F32, [TA])
            zlat = A.alloc(F32, [5, TA])
            sql = A.alloc(BF16, [5, TA])
            trl = [A.alloc(F32, [TA]) for _ in range(2)]
            qln_b = A.alloc(BF16, [3, TA])
            kvn_b = A.alloc(BF16, [2, TA])
            hpT = [[A.alloc(BF16, [TA]), A.alloc(BF16, [TA]), A.alloc(F32, [TA]), A.alloc(F32, [TA]),
                    A.alloc(F32, [TA])] for _ in range(2)]
            kpe_f = A.alloc(BF16, [TA])
            qblk = A.alloc(BF16, [8, TA], parts=96)
            cs1 = A.alloc(F32, [TA])
            bs1 = A.alloc(F32, [TA])
            ubuf = [A.alloc(F32, [TA + 2]) for _ in range(4)]
            cacc1 = A.alloc(F32, [TA])
            conv_u = A.alloc(F32, [4, TA])
            conv_n = A.alloc(BF16, [4, TA])
            trc = A.alloc(F32, [TA])
            ksq2 = [A.alloc(BF16, [TA]) for _ in range(2)]
            ktr2 = [A.alloc(F32, [TA]) for _ in range(2)]
            ktb = A.alloc(BF16, [8, TA], parts=96)
            vb = A.alloc(BF16, [TA // 128, 8 * 65])
            cosb2 = [A.alloc(F32, [TA], parts=96) for _ in range(2)]
            sinb2 = [A.alloc(F32, [TA], parts=96) for _ in range(2)]

            P.dma("sp", mk("dma_start", out=w_in_s.ap, in_=WIN_B[l].rearrange("p (kc n) -> p kc n", n=INC)),
                  reads=[dk("WIN", l, kc) for kc in range(8)], writes=[w_in_s])
            P.dma("sp", mk("dma_start", out=wk_s.ap, in_=WK_B[l].rearrange("p (kc n) -> p kc n", n=512)),
                  reads=[dk("WK", l, 0), dk("WK", l, 1)], writes=[wk_s])
            P.dma("sp", mk("dma_start", out=wv_s.ap, in_=WV_B[l].rearrange("p (kc n) -> p kc n", n=512)),
                  reads=[dk("WV", l, 0), dk("WV", l, 1)], writes=[wv_s])
            P.dma("sp", mk("dma_start", out=wuq_s.ap, in_=WUQ_B[l].rearrange("p (kc n) -> p kc n", n=768)),
                  reads=[dk("WUQ", l)], writes=[wuq_s])
            vb4 = vb.ap.rearrange("p a (h x) -> p a h x", x=65)
            P.c("pool", mk("memset", vb.ap, 1.0), writes=[vb])
            for cc in range(4):
                P.c("pool", mk("memset", ubuf[cc].ap[:, 0:2], 0.0), writes=[ubuf[cc]])

            def xviewA(ap, t):
                return ap.rearrange("(c p) t -> p c t", p=128)[:, :, t * TA:(t + 1) * TA]

            def loadcs(t):
                tc_ = slice(t * TA, (t + 1) * TA)
                P.dma("sp", mk("dma_start", out=cosb2[t % 2].ap, in_=COSD[:, tc_]), reads=[dk("COSD")], writes=[cosb2[t % 2]])
                P.dma("sp", mk("dma_start", out=sinb2[t % 2].ap, in_=SIND[:, tc_]), reads=[dk("SIND")], writes=[sinb2[t % 2]])

            def loadxA(t):
                xb = xb1
                P.dma("sp", mk("dma_start", out=xb.ap, in_=xviewA(xsrc(l), t)), reads=[dk("X", (t * TA) // TB)], writes=[xb])

            def c_normx(t):
                xb = xb1
                hbuf = hb2[t % 2]
                for c in range(8):
                    P.c("act", mk("activation", out=sqn.ap[:, c], in_=xb.ap[:, c], func=AF.Square),
                        reads=[xb.sub(c)], writes=[sqn.sub(c)])
                yield
                ps, pk = nextps()
                for c in range(8):
                    P.c("pe", mk("matmul", ps[:, 0:TA], lhsT=ONES, rhs=sqn.ap[:, c], start=(c == 0), stop=(c == 7)),
                        reads=[sqn.sub(c), cm_b], writes=[pk])
                rms_rstd(ps, pk, 1.0 / D, 128, trn, trn, TA)
                yield
                for c in range(8):
                    P.c("dve", mk("scalar_tensor_tensor",
                        out=hbuf.ap[:, c], in0=xb.ap[:, c], scalar=G(G_MIX, l, c), in1=trn.ap, op0=ALU.mult, op1=ALU.mult),
                        reads=[xb.sub(c), trn, gp], writes=[hbuf.sub(c)])
                if t + 1 < NBA:
                    loadxA(t + 1)

            def zmm(hbuf, ps, pk, c0, m):
                for kc in range(8):
                    P.c("pe", mk("matmul", ps[0:m, 0:TA], lhsT=w_in_s.ap[:, kc, c0:c0 + m], rhs=hbuf.ap[:, kc],
                                 start=(kc == 0), stop=(kc == 7)),
                        reads=[w_in_s, hbuf.sub(kc)], writes=[pk])

            def c_lat(hbuf, which):
                c0, nch, zoff, gk, scale, dstb = ((0, 3, 0, G_QL, 1.0 / 384, qln_b), (384, 2, 3, G_KVL, 1.0 / 256, kvn_b))[which]
                tmp = rstd = trl[which]
                for c in range(nch):
                    ps, pk = nextps()
                    zmm(hbuf, ps, pk, c0 + c * 128, 128)
                    P.c("act", mk("activation", out=zlat.ap[:, zoff + c], in_=ps[:, 0:TA], func=AF.Copy),
                        reads=[pk], writes=[zlat.sub(zoff + c)])
                    P.c("act", mk("activation", out=sql.ap[:, zoff + c], in_=ps[:, 0:TA], func=AF.Square),
                        reads=[pk], writes=[sql.sub(zoff + c)])
                    yield
                ps, pk = nextps()
                for c in range(nch):
                    P.c("pe", mk("matmul", ps[:, 0:TA], lhsT=ONES, rhs=sql.ap[:, zoff + c], start=(c == 0), stop=(c == nch - 1)),
                        reads=[sql.sub(zoff + c), cm_b], writes=[pk])
                rms_rstd(ps, pk, scale, 128, tmp, rstd, TA)
                yield
                for c in range(nch):
                    P.c("dve", mk("scalar_tensor_tensor",
                        out=dstb.ap[:, c], in0=zlat.ap[:, zoff + c], scalar=G(gk, l, c), in1=rstd.ap, op0=ALU.mult, op1=ALU.mult),
                        reads=[zlat.sub(zoff + c), rstd, gp], writes=[dstb.sub(c)])

            def c_kpe(hbuf, tcols, tok):
                ps, pk = nextps(hold=True)
                zmm(hbuf, ps, pk, 576, 96)
                yield from head_post_gen(ps, pk, G(G_KR, l, 0, 0, 96), tcols, kpe_f.ap[0:96], kpe_f, hpT[tok], True, TA)

            def c_qh(h, tcols, tok):
                ps, pk = nextps(hold=True)
                for kc in range(3):
                    P.c("pe", mk("matmul", ps[0:96, 0:TA], lhsT=wuq_s.ap[:, kc, h * 96:(h + 1) * 96], rhs=qln_b.ap[:, kc],
                                 start=(kc == 0), stop=(kc == 2)),
                        reads=[wuq_s, qln_b.sub(kc)], writes=[pk])
                yield from head_post_gen(ps, pk, G(G_QN, l, 0, 0, 96), tcols, qblk.ap[0:96, h], qblk.sub(h), hpT[tok], True, TA)

            def c_conv(hbuf, cc):
                cs, bs, cacc, ub = cs1, bs1, cacc1, ubuf[cc]
                psb, pkb = nextps()
                zmm(hbuf, psb, pkb, 672 + cc * 128, 128)
                P.c("act", mk("activation", out=bs.ap, in_=psb[:, 0:TA], func=AF.Copy), reads=[pkb], writes=[bs])
                yield
                psc, pkc = nextps()
                zmm(hbuf, psc, pkc, 1184 + cc * 128, 128)
                P.c("act", mk("activation", out=cs.ap, in_=psc[:, 0:TA], func=AF.Copy), reads=[pkc], writes=[cs])
                yield
                psx, pkx = nextps()
                zmm(hbuf, psx, pkx, 1696 + cc * 128, 128)
                P.c("dve", mk("tensor_tensor", out=ub.ap[:, 2:TA + 2], in0=cs.ap, in1=psx[:, 0:TA], op=ALU.mult),
                    reads=[cs, pkx], writes=[ub])
                yield
                P.c("act", mk("activation", out=cacc.ap, in_=ub.ap[:, 2:TA + 2], func=AF.Copy, scale=G(G_CW, l, 0 * 4 + cc)),
                    reads=[ub, gp], writes=[cacc])
                P.c("dve", mk("scalar_tensor_tensor", out=cacc.ap, in0=ub.ap[:, 1:TA + 1], scalar=G(G_CW, l, 1 * 4 + cc),
                              in1=cacc.ap, op0=ALU.mult, op1=ALU.add),
                    reads=[ub, gp, cacc], writes=[cacc])
                P.c("dve", mk("scalar_tensor_tensor", out=cacc.ap, in0=ub.ap[:, 0:TA], scalar=G(G_CW, l, 2 * 4 + cc),
                              in1=cacc.ap, op0=ALU.mult, op1=ALU.add),
                    reads=[ub, gp, cacc], writes=[cacc])
                P.c("pool", mk("tensor_tensor", out=conv_u.ap[:, cc], in0=cacc.ap, in1=bs.ap, op=ALU.mult),
                    reads=[cacc, bs], writes=[conv_u.sub(cc)])
                P.c("pool", mk("tensor_copy", out=ub.ap[:, 0:2], in_=ub.ap[:, TA:TA + 2]), reads=[ub], writes=[ub])

            def c_convnorm(t, tcols):
                for cc in range(4):
                    P.c("act", mk("activation", out=sqn.ap[:, cc], in_=conv_u.ap[:, cc], func=AF.Square),
                        reads=[conv_u.sub(cc)], writes=[sqn.sub(cc)])
                yield
                ps, pk = nextps()
                for cc in range(4):
                    P.c("pe", mk("matmul", ps[:, 0:TA], lhsT=ONES, rhs=sqn.ap[:, cc], start=(cc == 0), stop=(cc == 3)),
                        reads=[sqn.sub(cc), cm_b], writes=[pk])
                rms_rstd(ps, pk, 1.0 / 512, 128, trc, trc, TA)
                yield
                for cc in range(4):
                    P.c("dve", mk("scalar_tensor_tensor",
                        out=conv_n.ap[:, cc], in0=conv_u.ap[:, cc], scalar=G(G_OC, l, cc), in1=trc.ap, op0=ALU.mult, op1=ALU.mult),
                        reads=[conv_u.sub(cc), trc, gp], writes=[conv_n.sub(cc)])
                P.dma("sp", mk("dma_start", out=MIXC[:, :, tcols], in_=conv_n.ap), reads=[conv_n], writes=[dk("MIXC", t)])

            def c_kpair(jp, k):
                ksq = ksq2[k]
                ktmp = krstd = ktr2[k]
                ps, pk = nextps(hold=True)
                for kc in range(2):
                    P.c("pe", mk("matmul", ps[:, 0:TA], lhsT=wk_s.ap[:, kc, jp * 128:(jp + 1) * 128], rhs=kvn_b.ap[:, kc],
                                 start=(kc == 0), stop=(kc == 1)),
                        reads=[wk_s, kvn_b.sub(kc)], writes=[pk])
                P.c("act", mk("activation", out=ksq.ap, in_=ps[:, 0:TA], func=AF.Square), reads=[pk], writes=[ksq])
                yield
                ps2, pk2 = nextps()
                P.c("pe", mk("matmul", ps2[:, 0:TA], lhsT=BLK2, rhs=ksq.ap, start=True, stop=True),
                    reads=[ksq, cm_b], writes=[pk2])
                rms_rstd(ps2, pk2, 1.0, 128, ktmp, krstd, TA)
                yield
                for hh in range(2):
                    lo, hi = hh * 64, hh * 64 + 64
                    P.c("dve", mk("scalar_tensor_tensor",
                        out=ktb.ap[0:64, 2 * jp + hh], in0=ps[lo:hi, 0:TA], scalar=G(G_KN2, l, 0, lo, hi), in1=krstd.ap[lo:hi],
                        op0=ALU.mult, op1=ALU.mult),
                        reads=[pk, krstd, gp], writes=[ktb.sub(2 * jp + hh)])
                release(pk)

            def c_v(t):
                for i in range(TA // 128):
                    ps, pk = nextps()
                    for kc in range(2):
                        P.c("pe", mk("matmul", ps[:, :], lhsT=kvn_b.ap[:, kc, i * 128:(i + 1) * 128], rhs=wv_s.ap[:, kc],
                                     start=(kc == 0), stop=(kc == 1)),
                            reads=[wv_s, kvn_b.sub(kc)], writes=[pk])
                    P.c("act", mk("activation", out=vb4[:, i, :, 0:64], in_=ps[:, :].rearrange("p (h x) -> p h x", x=64), func=AF.Copy),
                        reads=[pk], writes=[vb.sub(i)])
                    yield
                nt = TA // 128
                P.dma("sp", mk("dma_start", out=VVD[:, nt * t:nt * t + nt, :], in_=vb.ap), reads=[vb], writes=[dk("VV", t)])

            def c_kfin(t, tcols):
                for h in range(8):
                    P.c("pool", mk("tensor_copy", out=ktb.ap[64:96, h], in_=kpe_f.ap[64:96]), reads=[kpe_f], writes=[ktb.sub(h)])
                P.dma("sp", mk("dma_start", out=KTD[:, :, tcols], in_=ktb.ap), reads=[ktb], writes=[dk("KT", t)])
                yield

            def c_qfin(t, tcols):
                P.dma("sp", mk("dma_start", out=QTD[:, :, tcols], in_=qblk.ap), reads=[qblk], writes=[dk("QT", t)])
                yield

            loadxA(0)
            loadcs(0)
            run_chains([c_normx(0)], 1)
            for t in range(NBA):
                tcols = (cosb2[t % 2], sinb2[t % 2])
                tsl = slice(t * TA, (t + 1) * TA)
                if t + 1 < NBA:
                    loadcs(t + 1)
                hbuf = hb2[t % 2]
                def QH(h):
                    return ("qh%d" % h, "hp", lambda tok, h=h: c_qh(h, tcols, tok), ["latq"])

                def KP(jp):
                    return ("kp%d" % jp, "ks", lambda tok, jp=jp: c_kpair(jp, tok), ["latkv"])

                def CV(cc):
                    return ("cv%d" % cc, "cv", lambda tok, cc=cc: c_conv(hbuf, cc), [])

                chains = [("latq", None, lambda tok: c_lat(hbuf, 0), []), ("latkv", None, lambda tok: c_lat(hbuf, 1), []),
                          ("kpe", "hp", lambda tok: c_kpe(hbuf, tcols, tok), []), CV(0)]
                if t + 1 < NBA:
                    chains.append(("normx", "sqn", lambda tok: c_normx(t + 1), []))
                chains += [CV(1), QH(0), KP(0), QH(1), CV(2), KP(1), QH(2), QH(3), CV(3), KP(2), QH(4), KP(3),
                           QH(5), ("v", None, lambda tok: c_v(t), ["latkv"]), QH(6),
                           ("convnorm", "sqn", lambda tok: c_convnorm(t, tsl), ["cv0", "cv1", "cv2", "cv3"]), QH(7)]
                run_chains(chains, WCH, {"hp": list(range(K_HP)), "ks": list(range(K_KS)), "cv": [0], "sqn": [0]})
                run_chains([c_kfin(t, tsl), c_qfin(t, tsl)], 2)
                if l == 0 and t == 0:
                    emit_casts(0, "C")

            A.reset(const_mark)
            kt_s = A.alloc(BF16, [8, S], parts=96)
            vv_s = A.alloc(BF16, [32, 8 * 65])
            qblk2 = [A.alloc(BF16, [8, TB], parts=96) for _ in range(2)]
            ptb = [A.alloc(BF16, [TB]) for _ in range(K_LA + 2)]
            osb = A.alloc(F32, [TB])
            rden = A.alloc(F32, [TB])
            attn_u = A.alloc(F32, [4, TB])
            attn_n = A.alloc(BF16, [4, TB])
            sqB = [A.alloc(BF16, [TB]) for _ in range(4)]
            tmpB = A.alloc(F32, [TB])
            rstdB = A.alloc(F32, [TB])
            vv4 = vv_s.ap.rearrange("p a (h x) -> p a h x", x=65)

            def loadq(j):
                qb = qblk2[j % 2]
                P.dma("sp", mk("dma_start", out=qb.ap, in_=QTD[:, :, j * TB:(j + 1) * TB]), reads=[dk("QT", i) for i in range(j * RAB, (j + 1) * RAB)], writes=[qb])

            def loadkv(j):
                P.dma("sp", mk("dma_start", out=kt_s.ap[:, :, j * TB:(j + 1) * TB], in_=KTD[:, :, j * TB:(j + 1) * TB]),
                      reads=[dk("KT", i) for i in range(j * RAB, (j + 1) * RAB)], writes=[kt_s] if j == 0 else [[("kt", j)]])
                P.dma("sp", mk("dma_start", out=vv_s.ap[:, 4 * j:4 * j + 4], in_=VVD[:, 4 * j:4 * j + 4, :]),
                      reads=[dk("VV", i) for i in range(j * RAB, (j + 1) * RAB)], writes=[vv_s] if j == 0 else [[("vv", j)]])

            loadq(0)
            loadkv(0)
            if l + 1 < nl:
                defer_casts[0] = True
                emit_casts(l + 1, "A")
                emit_casts(l + 1, "C")
                defer_casts[0] = False
            SCALE = 96.0 ** -0.5

            def tail_chain(h, pso, pko, j=0):
                P.c("dve", mk("tensor_copy", out=osb.ap[0:65], in_=pso[0:65, :]), reads=[pko], writes=[osb])
                release(pko)
                if j >= 2:
                    P.c("dve", mk("reciprocal", out=rden.ap[64:65], in_=osb.ap[64:65]), reads=[osb], writes=[rden])
                    yield
                else:
                    P.c("act", mk("activation", out=rden.ap[64:65], in_=osb.ap[64:65], func=AF.Ln), reads=[osb], writes=[rden])
                    P.c("act", mk("activation", out=rden.ap[64:65], in_=rden.ap[64:65], func=AF.Exp, scale=-1.0), reads=[rden], writes=[rden])
                yield
                psb_, pkb_ = nextps()
                P.c("pe", mk("matmul", psb_[0:64, :], lhsT=ONESF[64:65, 0:64], rhs=rden.ap[64:65], start=True, stop=True),
                    reads=[rden, cm_f], writes=[pkb_])
                plo = (h % 2) * 64
                P.c("dve", mk("tensor_tensor", out=attn_u.ap[plo:plo + 64, h // 2], in0=osb.ap[0:64], in1=psb_[0:64, :], op=ALU.mult),
                    reads=[osb, pkb_], writes=[attn_u.sub(h // 2)])

            def block_tail(j):
                for c in range(4):
                    P.c("pool", mk("tensor_tensor", out=sqB[c].ap, in0=attn_u.ap[:, c], in1=attn_u.ap[:, c], op=ALU.mult),
                        reads=[attn_u.sub(c)], writes=[sqB[c]])
                yield
                yield
                ps, pk = nextps()
                for c in range(4):
                    P.c("pe", mk("matmul", ps[:, :], lhsT=ONES, rhs=sqB[c].ap, start=(c == 0), stop=(c == 3)),
                        reads=[sqB[c], cm_b], writes=[pk])
                rms_rstd(ps, pk, 1.0 / 512, 128, tmpB, rstdB)
                yield
                for c in range(4):
                    P.c("dve", mk("scalar_tensor_tensor",
                        out=attn_n.ap[:, c], in0=attn_u.ap[:, c], scalar=G(G_OA, l, c), in1=rstdB.ap,
                        op0=ALU.mult, op1=ALU.mult),
                        reads=[attn_u.sub(c), rstdB, gp], writes=[attn_n.sub(c)])
                P.dma("sp", mk("dma_start", out=MIXA[:, :, j * TB:(j + 1) * TB], in_=attn_n.ap), reads=[attn_n], writes=[dk("MIXA", j)])

            for j in range(NB):
                if j + 1 < NB:
                    loadq(j + 1)
                    loadkv(j + 1)
                nkc = 4 * j + 4
                qb = qblk2[j % 2]
                tasks = [(h, kc) for h in range(8) for kc in range(nkc)]
                stiles = {}
                psos = {}
                LA = K_LA

                def issue_s(i):
                    h, kc = tasks[i]
                    r = kc - 4 * j
                    qoff = 128 * r if r > 0 else 0
                    n = TB - qoff
                    pss, pks = nextps(hold=True)
                    kdeps = [kt_s] if kc < 4 else [[("kt", kc // 4)]]
                    P.c("pe", mk("matmul",
                        pss[:, 0:n], lhsT=kt_s.ap[:, h, kc * 128:(kc + 1) * 128], rhs=qb.ap[0:96, h, qoff:TB], start=True, stop=True),
                        reads=kdeps + [qb.sub(h)], writes=[pks])
                    stiles[i] = (pss, pks, r, qoff, n)

                for i in range(min(LA, len(tasks))):
                    issue_s(i)
                for i, (h, kc) in enumerate(tasks):
                    if kc == 0:
                        if j >= 1:
                            flush_casts(1 if j < NB - 1 else 1000)
                        while len(held) - len(stiles) > 1:
                            step()
                        psos[h] = nextps(hold=True)
                    pso, pko = psos[h]
                    pss, pks, r, qoff, n = stiles.pop(i)
                    pt = ptb[i % (K_LA + 2)]
                    vdeps = [vv_s] if kc < 4 else [[("vv", kc // 4)]]
                    P.c("act", mk("activation", out=pt.ap[:, 0:n], in_=pss[:, 0:n], func=AF.Exp, scale=SCALE),
                        reads=[pks], writes=[pt])
                    release(pks)
                    if r >= 0:
                        P.c("pool", mk("tensor_tensor", out=pt.ap[:, 0:128], in0=pt.ap[:, 0:128], in1=TRI, op=ALU.mult),
                            reads=[pt, cm_b], writes=[pt])
                    if i + LA < len(tasks):
                        issue_s(i + LA)
                    P.c("pe", mk("matmul",
                        pso[0:65, qoff:TB], lhsT=vv4[:, kc, h, :], rhs=pt.ap[:, 0:n], start=(kc == 0), stop=(kc == nkc - 1)),
                        reads=vdeps + [pt], writes=[pko])
                    if i % K_STEP == K_STEP - 1:
                        step()
                    if kc == nkc - 1:
                        if h == 7:
                            drain()
                        bg.append(tail_chain(h, pso, pko, j))
                        if h == 7:
                            bg.append(block_tail(j))
            drain()

            A.reset(const_mark)
            NRING = 6
            ring = [A.alloc(BF16, [PIECE]) for _ in range(NRING)]
            xc2 = [A.alloc(F32, [8, TB]) for _ in range(2)]
            hC = A.alloc(BF16, [8, TB])
            hCb = A.alloc(BF16, [8, TB])
            sqC = [A.alloc(BF16, [TB]) for _ in range(3)]
            uC = A.alloc(BF16, [32, TB])
            mixa2 = [A.alloc(BF16, [4, TB]) for _ in range(2)]
            mixc2 = [A.alloc(BF16, [4, TB]) for _ in range(2)]
            pb2 = [A.alloc(BF16, [2, TB]) for _ in range(2)]
            gate = A.alloc(F32, [TB])
            gtmp = A.alloc(F32, [TB])
            relu2 = [A.alloc(F32, [TB]) for _ in range(2)]
            tmpC = A.alloc(F32, [TB])
            rstdC = A.alloc(F32, [TB])
            ringctr = [0]

            def loadpiece(i):
                slot = ring[ringctr[0] % NRING]
                ringctr[0] += 1
                if i == 20:
                    P.dma("sp", mk("dma_start", out=slot.ap[:, 0:2048], in_=WC_B[l, i][:, 0:2048]), reads=[dk("WC", l, i)], writes=[slot])
                else:
                    P.dma("sp", mk("dma_start", out=slot.ap, in_=WC_B[l, i]), reads=[dk("WC", l, i)], writes=[slot])
                return slot

            def loadblk(t):
                xb = xc2[t % 2]
                P.dma("sp", mk("dma_start", out=xb.ap, in_=xview(xsrc(l), t)), reads=[dk("X", t)], writes=[xb])
                ma = mixa2[t % 2]
                P.dma("sp", mk("dma_start", out=ma.ap, in_=MIXA[:, :, t * TB:(t + 1) * TB]), reads=[dk("MIXA", t)], writes=[ma])
                mc = mixc2[t % 2]
                P.dma("sp", mk("dma_start", out=mc.ap, in_=MIXC[:, :, t * TB:(t + 1) * TB]), reads=[dk("MIXC", i) for i in range(t * RAB, (t + 1) * RAB)], writes=[mc])
                pb = pb2[t % 2]
                P.dma("pool", mk("dma_start", out=pb.ap, in_=pT[l].rearrange("(kc p) t -> p kc t", p=128)[:, :, t * TB:(t + 1) * TB]),
                      writes=[pb])

            W_P, U_P, D_P, G_P = [0, 1], list(range(2, 10)), list(range(10, 18)), [18, 19, 20]
            sched = [(0, i) for i in W_P]
            for t in range(NB):
                sched += [(t, i) for i in U_P] + [(t, i) for i in D_P]
                if t + 1 < NB:
                    sched += [(t + 1, i) for i in W_P]
                sched += [(t, i) for i in G_P]
            pos = {k: n for n, k in enumerate(sched)}
            loaded = {}
            nload = [0]

            def ensure(upto):
                while nload[0] < len(sched) and nload[0] <= upto:
                    tt, ii = sched[nload[0]]
                    loaded[(tt, ii)] = loadpiece(ii)
                    nload[0] += 1

            def piece(t, i):
                ensure(pos[(t, i)])
                return loaded[(t, i)]

            def done(t, i):
                ensure(pos[(t, i)] + NRING)

            hC2 = [hC, hCb]

            def nacc_begin():
                nps, npk = nextps(hold=True)
                return {"ps": nps, "pk": npk}

            def nacc_mm(st, c):
                sq = sqC[c % 3]
                P.c("pe", mk("matmul", st["ps"][:, :], lhsT=ONES, rhs=sq.ap, start=(c == 0), stop=(c == 7)),
                    reads=[sq, cm_b], writes=[st["pk"]])

            def nacc_chunk(st, xb, c):
                sq = sqC[c % 3]
                P.c("act", mk("activation", out=sq.ap, in_=xb.ap[:, c], func=AF.Square), reads=[xb.sub(c)], writes=[sq])

            def nacc_finish(st, xb, gkind, hdst):
                nacc_mm(st, 7)
                rms_rstd(st["ps"], st["pk"], 1.0 / D, 128, tmpC, rstdC)
                release(st["pk"])
                for c in range(8):
                    P.c("dve", mk("scalar_tensor_tensor",
                        out=hdst.ap[:, c], in0=xb.ap[:, c], scalar=G(gkind, l, c), in1=rstdC.ap, op0=ALU.mult, op1=ALU.mult),
                        reads=[xb.sub(c), rstdC, gp], writes=[hdst.sub(c)])

            def sec_W(t):
                xb, ma, mc = xc2[t % 2], mixa2[t % 2], mixc2[t % 2]
                po0, po1 = piece(t, 0), piece(t, 1)
                wo0 = po0.ap.rearrange("p (c n) -> p c n", n=1024)
                wo1 = po1.ap.rearrange("p (c n) -> p c n", n=1024)
                nst = nacc_begin()
                for oc in range(8):
                    ps, pk = nextps()
                    ocs = slice(oc * 128, (oc + 1) * 128)
                    for c in range(4):
                        P.c("pe", mk("matmul", ps[:, :], lhsT=wo0[:, c, ocs], rhs=ma.ap[:, c], start=(c == 0), stop=False),
                            reads=[po0, ma.sub(c)], writes=[pk])
                    for c in range(4):
                        P.c("pe", mk("matmul", ps[:, :], lhsT=wo1[:, c, ocs], rhs=mc.ap[:, c], start=False, stop=(c == 3)),
                            reads=[po1, mc.sub(c)], writes=[pk])
                    if oc >= 1:
                        nacc_mm(nst, oc - 1)
                    P.c("dve", mk("tensor_tensor", out=xb.ap[:, oc], in0=xb.ap[:, oc], in1=ps[:, :], op=ALU.add),
                        reads=[xb.sub(oc), pk], writes=[xb.sub(oc)])
                    nacc_chunk(nst, xb, oc)
                done(t, 0)
                done(t, 1)
                nacc_finish(nst, xb, G_MLP, hC2[0])

            def sec_U(t):
                h2 = hC2[0]
                for g in range(8):
                    pc = piece(t, 2 + g)
                    wup = pc.ap.rearrange("p (kc n) -> p kc n", n=512)
                    for f in range(4):
                        ps, pk = nextps()
                        for kc in range(8):
                            P.c("pe", mk("matmul", ps[:, :], lhsT=wup[:, kc, f * 128:(f + 1) * 128], rhs=h2.ap[:, kc],
                                         start=(kc == 0), stop=(kc == 7)),
                                reads=[pc, h2.sub(kc)], writes=[pk])
                        rl = relu2[(4 * g + f) % 2]
                        P.c("act", mk("activation", out=rl.ap, in_=ps[:, :], func=AF.Relu), reads=[pk], writes=[rl])
                        P.c("dve", mk("tensor_tensor", out=uC.ap[:, 4 * g + f], in0=rl.ap, in1=rl.ap, op=ALU.mult),
                            reads=[rl], writes=[uC.sub(4 * g + f)])
                    done(t, 2 + g)

            def sec_D(t):
                xb = xc2[t % 2]
                nst3 = nacc_begin()
                for oc in range(8):
                    pc = piece(t, 10 + oc)
                    wdn = pc.ap.rearrange("p (fc n) -> p fc n", n=128)
                    ps, pk = nextps()
                    for fc in range(32):
                        P.c("pe", mk("matmul", ps[:, :], lhsT=wdn[:, fc, :], rhs=uC.ap[:, fc], start=(fc == 0), stop=(fc == 31)),
                            reads=[pc, uC.sub(fc)], writes=[pk])
                    if oc >= 1:
                        nacc_mm(nst3, oc - 1)
                    P.c("dve", mk("tensor_tensor", out=xb.ap[:, oc], in0=xb.ap[:, oc], in1=ps[:, :], op=ALU.add),
                        reads=[xb.sub(oc), pk], writes=[xb.sub(oc)])
                    nacc_chunk(nst3, xb, oc)
                    done(t, 10 + oc)
                nacc_finish(nst3, xb, G_PLE, hC2[1])

            def sec_G(t):
                xb, pb, h3 = xc2[t % 2], pb2[t % 2], hC2[1]
                pg = [piece(t, 18), piece(t, 19)]
                pp = piece(t, 20)
                wpl = pp.ap[:, 0:2048].rearrange("p (kc n) -> p kc n", n=1024)
                for oc in range(8):
                    pgi = pg[oc // 4]
                    wgv = pgi.ap.rearrange("p (kc n) -> p kc n", n=512)
                    o4 = (oc % 4) * 128
                    ps, pk = nextps()
                    for kc in range(8):
                        P.c("pe", mk("matmul", ps[:, :], lhsT=wgv[:, kc, o4:o4 + 128], rhs=h3.ap[:, kc],
                                     start=(kc == 0), stop=(kc == 7)),
                            reads=[pgi, h3.sub(kc)], writes=[pk])
                    P.c("act", mk("activation", out=gate.ap, in_=ps[:, :], func=AF.Sigmoid), reads=[pk], writes=[gate])
                    ps2, pk2 = nextps()
                    for kc in range(2):
                        P.c("pe", mk("matmul", ps2[:, :], lhsT=wpl[:, kc, oc * 128:(oc + 1) * 128], rhs=pb.ap[:, kc],
                                     start=(kc == 0), stop=(kc == 1)),
                            reads=[pp, pb.sub(kc)], writes=[pk2])
                    P.c("dve", mk("tensor_tensor", out=gtmp.ap, in0=gate.ap, in1=ps2[:, :], op=ALU.mult),
                        reads=[gate, pk2], writes=[gtmp])
                    P.c("pool", mk("tensor_tensor", out=xb.ap[:, oc], in0=xb.ap[:, oc], in1=gtmp.ap, op=ALU.add),
                        reads=[xb.sub(oc), gtmp], writes=[xb.sub(oc)])
                done(t, 18)
                done(t, 19)
                done(t, 20)
                so = P.dma("sp", mk("dma_start", out=xview(outT, t), in_=xb.ap), reads=[xb], writes=[dk("X", t)])
                if l == nl - 1:
                    out_stores.append(so)

            loadblk(0)
            if NB > 1:
                loadblk(1)
            ensure(NRING - 1)
            sec_W(0)
            for t in range(NB):
                sec_U(t)
                sec_D(t)
                if t + 1 < NB:
                    sec_W(t + 1)
                sec_G(t)
                if t + 2 < NB:
                    loadblk(t + 2)

        P.emit(final_ops=out_stores)
        build.stats = P.stats
    return nc


def _gpack(inp):
    g = np.zeros((128, NG), np.float32)
    for l in range(NL):
        for kind, name in ((G_MIX, "g_mix"), (G_MLP, "g_mlp"), (G_PLE, "g_ple")):
            for c in range(8):
                g[:, gcol(kind, l, c)] = inp[name][l, c * 128:(c + 1) * 128]
        for c in range(3):
            g[:, gcol(G_QL, l, c)] = inp["g_q_lat"][l, c * 128:(c + 1) * 128]
        for c in range(2):
            g[:, gcol(G_KVL, l, c)] = inp["g_kv_lat"][l, c * 128:(c + 1) * 128]
        for c in range(4):
            g[:, gcol(G_OC, l, c)] = inp["g_out_conv"][l, c * 128:(c + 1) * 128]
        for j in range(3):
            for c in range(4):
                g[:, gcol(G_CW, l, j * 4 + c)] = inp["conv_w"][l, j, c * 128:(c + 1) * 128]
        for c in range(4):
            g[:, gcol(G_OA, l, c)] = inp["g_out_attn"][l, c * 128:(c + 1) * 128]
        g[0:64, gcol(G_QN, l)] = inp["g_qn_nope"][l]
        g[64:96, gcol(G_QN, l)] = inp["g_qn_rope"][l]
        g[0:64, gcol(G_KN2, l)] = inp["g_kn_nope"][l]
        g[64:128, gcol(G_KN2, l)] = inp["g_kn_nope"][l]
        g[64:96, gcol(G_KR, l)] = inp["g_kn_rope"][l]
    return g


def _cmat():
    c = np.zeros((128, NCM), np.float32)
    c[:, C_ONES:C_ONES + 128] = 1.0
    c[0:64, C_BLK96:C_BLK96 + 64] = 1.0 / 64
    c[64:96, C_BLK96 + 64:C_BLK96 + 96] = 1.0 / 32
    c[0:64, C_BLK2:C_BLK2 + 64] = 1.0 / 64
    c[64:128, C_BLK2 + 64:C_BLK2 + 128] = 1.0 / 64
    for i in range(16):
        c[80 + i, C_ROT + 64 + i] = -1.0
        c[64 + i, C_ROT + 80 + i] = 1.0
    kq = np.arange(128)
    c[:, C_TRI:C_TRI + 128] = (kq[None, :] >= kq[:, None]).astype(np.float32)
    f = (1.0 / (10000.0 ** (np.arange(0, 32, 2, dtype=np.float32) / 32))).astype(np.float32)
    c[64:80, C_INVF] = f
    c[80:96, C_INVF] = f
    return c


_NC_CACHE = {}


def kernel(**inp):
    inp = {k: np.asarray(v) for k, v in inp.items()}
    if "nc" not in _NC_CACHE:
        _NC_CACHE["nc"] = build()
    nc = _NC_CACHE["nc"]
    gp = _gpack(inp)
    cm = _cmat()
    shared = {k: np.ascontiguousarray(inp[k], dtype=np.float32) for k in
              ("w_in", "w_uq", "w_ukv", "w_o", "w_up", "w_down", "w_ple_gate", "w_ple")}
    in_maps = []
    for b in range(8):
        m = dict(shared)
        m["xT"] = np.ascontiguousarray(inp["x"][b].T)
        m["pT"] = np.ascontiguousarray(np.transpose(inp["p"][:, b], (0, 2, 1)))
        m["pos"] = np.ascontiguousarray(inp["positions"][b][None, :].astype(np.int32))
        m["gpack"] = gp
        m["cmat"] = cm
        in_maps.append(m)
    res = run_bass_kernel_spmd(nc, in_maps, core_ids=list(range(8)))
    out = np.stack([np.ascontiguousarray(r["outT"].T) for r in res.results], axis=0)
    return out.astype(np.float32)
```

```python
import contextlib
import numpy as np
import concourse.bass as bass
import concourse.mybir as mybir
from concourse.bass_utils import run_bass_kernel_spmd

F32 = mybir.dt.float32
BF16 = mybir.dt.bfloat16
I32 = mybir.dt.int32
ALU = mybir.AluOpType
AF = mybir.ActivationFunctionType

S = 4096
D = 1024
NL = 4
TB = 512
NB = S // TB
TA = 512
NBA = S // TA
WCH = 6
K_HP = 2
K_KS = 2
K_LA = 4
K_STEP = 4
RAB = TB // TA
EPS = 1e-6
INC = 2208
NPIECE = 21
PIECE = 4096

G_MIX, G_MLP, G_PLE, G_QL, G_KVL, G_OC, G_CW, G_OA, G_QN, G_KN2, G_KR = range(11)
_GW = [8, 8, 8, 3, 2, 4, 12, 8, 1, 1, 1]
_GOFF = np.concatenate([[0], np.cumsum([w * NL for w in _GW])]).astype(int)
NG = int(_GOFF[-1])


def gcol(kind, l, i=0):
    return int(_GOFF[kind] + l * _GW[kind] + i)


C_ONES, C_BLK96, C_BLK2, C_ROT, C_TRI = 0, 128, 256, 384, 512
C_INVF = 640
NCM = 641


class Op:
    __slots__ = ("eng", "fn", "kind", "deps", "signal", "sem", "sigval", "waits", "idx")

    def __init__(self, eng, fn, kind):
        self.eng = eng
        self.fn = fn
        self.kind = kind
        self.deps = []
        self.signal = False
        self.sem = None
        self.sigval = 0
        self.waits = []


SAME_ENG_FULL = ("pool", "dve", "act")


class Prog:
    ENGS = ("pe", "act", "dve", "pool", "sp")
    NSLOT = {"pe": 2, "act": 2, "dve": 2, "pool": 64, "sp": 24}

    def __init__(self, nc):
        self.nc = nc
        self.ops = []
        self.last_writer = {}
        self.readers = {}

    def _add(self, eng, fn, kind, reads, writes):
        op = Op(eng, fn, kind)
        op.idx = len(self.ops)
        deps = {}
        rk = []
        for r in reads:
            rk.extend(r)
        wk = []
        for w in writes:
            wk.extend(w)
        for k in rk:
            w = self.last_writer.get(k)
            if w is not None:
                deps[w.idx] = (w, True)
        for k in wk:
            w = self.last_writer.get(k)
            if w is not None and w.idx not in deps:
                deps[w.idx] = (w, False)
            for r in self.readers.get(k, ()):
                if r.idx not in deps:
                    deps[r.idx] = (r, False)
        for (d, raw) in deps.values():
            if d.kind == "c" and kind == "c" and d.eng == eng:
                if eng == "pe" or (not raw and eng not in SAME_ENG_FULL):
                    continue
            op.deps.append(d)
            d.signal = True
        for k in wk:
            self.last_writer[k] = op
            self.readers[k] = []
        for k in rk:
            self.readers.setdefault(k, []).append(op)
        self.ops.append(op)
        return op

    def c(self, eng, fn, reads=(), writes=()):
        return self._add(eng, fn, "c", reads, writes)

    def dma(self, eng, fn, reads=(), writes=()):
        op = self._add(eng, fn, "d", reads, writes)
        op.signal = True
        return op

    def emit(self, final_ops=(), final_eng="sp"):
        nc = self.nc
        with contextlib.ExitStack() as st:
            csem = {e: st.enter_context(nc.semaphore("c_" + e)) for e in self.ENGS}
            dsem = {e: [st.enter_context(nc.semaphore("d_%s_%d" % (e, i))) for i in range(self.NSLOT[e])]
                    for e in self.ENGS}
            ccount = {e: 0 for e in self.ENGS}
            dcount = {e: 0 for e in self.ENGS}
            slotlast = {e: [None] * self.NSLOT[e] for e in self.ENGS}
            slotcnt = {e: [0] * self.NSLOT[e] for e in self.ENGS}
            waited = {e: {} for e in self.ENGS}
            for op in self.ops:
                w = waited[op.eng]
                waits = []

                def need(sem, val):
                    if w.get(id(sem), 0) < val:
                        w[id(sem)] = val
                        waits.append((sem, val))

                for d in op.deps:
                    need(d.sem, d.sigval)
                if op.kind == "d":
                    e = op.eng
                    s = dcount[e] % self.NSLOT[e]
                    dcount[e] += 1
                    prev = slotlast[e][s]
                    if prev is not None:
                        need(prev.sem, prev.sigval)
                    slotcnt[e][s] += 1
                    op.sem = dsem[e][s]
                    op.sigval = 16 * slotcnt[e][s]
                    slotlast[e][s] = op
                elif op.signal:
                    ccount[op.eng] += 1
                    op.sem = csem[op.eng]
                    op.sigval = ccount[op.eng]
                op.waits = waits
            fin = []
            wf = waited[final_eng]
            for d in final_ops:
                if wf.get(id(d.sem), 0) < d.sigval:
                    wf[id(d.sem)] = d.sigval
                    fin.append((d.sem, d.sigval))
            per = {e: [o for o in self.ops if o.eng == e] for e in self.ENGS}
            self.stats = {e: len(per[e]) for e in self.ENGS}
            self.stats["sig"] = dict(ccount)

            def run(eng_obj, e):
                for op in per[e]:
                    for (sem, val) in op.waits:
                        eng_obj.wait_ge(sem, val)
                    ins = op.fn(eng_obj)
                    if op.signal:
                        ins.then_inc(op.sem, 16 if op.kind == "d" else 1)
                if e == final_eng:
                    for (sem, val) in fin:
                        eng_obj.wait_ge(sem, val)

            with nc.Block() as block:
                @block.tensor
                def _(eng):
                    run(eng, "pe")

                @block.scalar
                def _(eng):
                    run(eng, "act")

                @block.vector
                def _(eng):
                    run(eng, "dve")

                @block.gpsimd
                def _(eng):
                    run(eng, "pool")

                @block.sync
                def _(eng):
                    run(eng, "sp")


PAGE = 256


def mk(method, *args, **kw):
    return lambda e: getattr(e, method)(*args, **kw)


class Reg:
    def __init__(self, ap, lo, hi):
        self.ap = ap
        self.lo = lo
        self.hi = hi
        self.keys = list(range(lo // PAGE, (hi - 1) // PAGE + 1))

    def __iter__(self):
        return iter(self.keys)

    def sub(self, i, n=1):
        nfirst = self.ap.shape[1]
        step = (self.hi - self.lo) // nfirst
        ap = self.ap[:, i] if n == 1 else self.ap[:, i:i + n]
        return Reg(ap, self.lo + i * step, self.lo + (i + n) * step)


class Arena:
    def __init__(self, tile, nbytes):
        self.t = tile
        self.n = nbytes
        self.off = 0
        self.marks = []

    def alloc(self, dtype, free, parts=128):
        es = 4 if dtype in (F32, I32) else 2
        n = es
        for f in free:
            n *= f
        lo = (self.off + 255) // 256 * 256
        hi = lo + n
        assert hi <= self.n, "arena overflow %d > %d" % (hi, self.n)
        self.off = hi
        ap = self.t[0:parts, lo // 2:hi // 2]
        if dtype != BF16:
            ap = ap.bitcast(dtype)
        if len(free) == 2:
            ap = ap.rearrange("p (a b) -> p a b", b=free[1])
        elif len(free) == 3:
            ap = ap.rearrange("p (a b c) -> p a b c", b=free[1], c=free[2])
        return Reg(ap, lo, hi)

    def mark(self):
        return self.off

    def reset(self, m):
        self.off = m


def build(nl=NL, dbg=False):
    nc = bass.Bass("TRN2", target_bir_lowering=False)
    dt_in = lambda name, shape, dt=F32: nc.dram_tensor(name, shape, dt, kind="ExternalInput").ap()
    xT = dt_in("xT", [D, S])
    pT = dt_in("pT", [NL, 256, S])
    pos = dt_in("pos", [1, S], I32)
    w_in = dt_in("w_in", [NL, D, INC])
    w_uq = dt_in("w_uq", [NL, 384, 768])
    w_ukv = dt_in("w_ukv", [NL, 256, 1024])
    w_o = dt_in("w_o", [NL, D, D])
    w_up = dt_in("w_up", [NL, D, 4096])
    w_down = dt_in("w_down", [NL, 4096, D])
    w_g = dt_in("w_ple_gate", [NL, D, D])
    w_ple = dt_in("w_ple", [NL, 256, D])
    gpack = dt_in("gpack", [128, NG])
    cmat = dt_in("cmat", [128, NCM])
    outT = nc.dram_tensor("outT", [D, S], F32, kind="ExternalOutput").ap()

    WIN_B = nc.dram_tensor("WIN_B", [NL, 128, 8 * INC], BF16).ap()
    WK_B = nc.dram_tensor("WK_B", [NL, 128, 1024], BF16).ap()
    WV_B = nc.dram_tensor("WV_B", [NL, 128, 1024], BF16).ap()
    WUQ_B = nc.dram_tensor("WUQ_B", [NL, 128, 3 * 768], BF16).ap()
    WC_B = nc.dram_tensor("WC_B", [NL, NPIECE, 128, PIECE], BF16).ap()
    sk = "ExternalOutput" if dbg else "Internal"
    QTD = nc.dram_tensor("QTD", [96, 8, S], BF16, kind=sk).ap()
    KTD = nc.dram_tensor("KTD", [96, 8, S], BF16, kind=sk).ap()
    VVD = nc.dram_tensor("VVD", [128, 32, 8 * 65], BF16, kind=sk).ap()
    MIXA = nc.dram_tensor("MIXA", [128, 4, S], BF16, kind=sk).ap()
    MIXC = nc.dram_tensor("MIXC", [128, 4, S], BF16, kind=sk).ap()
    COSD = nc.dram_tensor("COSD", [96, S], F32).ap()
    SIND = nc.dram_tensor("SIND", [96, S], F32).ap()

    ARENA_BYTES = 204 * 1024
    with contextlib.ExitStack() as st:
        arena_t = st.enter_context(nc.sbuf_tensor("arena", [128, ARENA_BYTES // 2], BF16))
        pst = [st.enter_context(nc.psum_tensor("ps%d" % i, [128, 512], F32)) for i in range(8)]
        A = Arena(arena_t, ARENA_BYTES)
        P = Prog(nc)
        psk = [[("ps", i)] for i in range(8)]
        psctr = [0]

        held = set()

        def nextps(hold=False):
            while True:
                i = psctr[0] % 8
                psctr[0] += 1
                if i not in held:
                    break
            if hold:
                held.add(i)
            return pst[i], psk[i]

        def release(pk):
            held.discard(pk[0][1])

        def dk(name, *idx):
            return [("d", name) + tuple(idx)]

        gp = A.alloc(F32, [NG])
        cm_f = A.alloc(F32, [NCM])
        cm_b = A.alloc(BF16, [640])
        epsc = A.alloc(F32, [1])
        P.dma("sp", mk("dma_start", out=gp.ap, in_=gpack), writes=[gp])
        P.dma("sp", mk("dma_start", out=cm_f.ap, in_=cmat), writes=[cm_f])
        P.c("dve", mk("tensor_copy", out=cm_b.ap, in_=cm_f.ap[:, 0:640]), reads=[cm_f], writes=[cm_b])
        P.c("dve", mk("memset", epsc.ap, EPS), writes=[epsc])
        ONES = cm_b.ap[:, C_ONES:C_ONES + 128]
        BLK96 = cm_b.ap[0:96, C_BLK96:C_BLK96 + 96]
        BLK2 = cm_b.ap[:, C_BLK2:C_BLK2 + 128]
        ROT = cm_b.ap[0:96, C_ROT:C_ROT + 96]
        TRI = cm_b.ap[:, C_TRI:C_TRI + 128]
        ONESF = cm_f.ap[:, C_ONES:C_ONES + 64]

        def G(kind, l, i=0, lo=0, hi=128):
            c = gcol(kind, l, i)
            return gp.ap[lo:hi, c:c + 1]

        cast_q = []
        defer_casts = [False]

        def pdma_cast(fn, writes):
            if defer_casts[0]:
                cast_q.append((fn, writes))
            else:
                P.dma("pool", fn, writes=writes)

        def flush_casts(n):
            for _ in range(min(n, len(cast_q))):
                fn, writes = cast_q.pop(0)
                P.dma("pool", fn, writes=writes)

        def emit_casts(l, part):
            if part == "A":
                emit_casts_a(l)
            else:
                emit_casts_c(l)

        def emit_casts_a(l):
            v = w_in[l].rearrange("(kc p) n -> p kc n", p=128)
            dv = WIN_B[l].rearrange("p (kc n) -> p kc n", n=INC)
            for kc in range(8):
                pdma_cast(mk("dma_start", out=dv[:, kc], in_=v[:, kc]), [dk("WIN", l, kc)])
            v = w_ukv[l].rearrange("(kc p) (h x) -> p kc h x", p=128, x=128)
            for kc in range(2):
                pdma_cast(mk("dma_start",
                    out=WK_B[l].rearrange("p (kc h x) -> p kc h x", kc=2, x=64)[:, kc], in_=v[:, kc, :, 0:64]), [dk("WK", l, kc)])
                pdma_cast(mk("dma_start",
                    out=WV_B[l].rearrange("p (kc h x) -> p kc h x", kc=2, x=64)[:, kc], in_=v[:, kc, :, 64:128]), [dk("WV", l, kc)])
            pdma_cast(mk("dma_start",
                out=WUQ_B[l].rearrange("p (kc n) -> p kc n", n=768),
                in_=w_uq[l].rearrange("(kc p) n -> p kc n", p=128)), [dk("WUQ", l)])

        def emit_casts_c(l):
            for i in range(2):
                pdma_cast(mk("dma_start",
                    out=WC_B[l, i].rearrange("p (c n) -> p c n", n=1024),
                    in_=w_o[l][512 * i:512 * (i + 1)].rearrange("(c p) n -> p c n", p=128)), [dk("WC", l, i)])
            vu = w_up[l].rearrange("(kc p) n -> p kc n", p=128)
            for g in range(8):
                pdma_cast(mk("dma_start",
                    out=WC_B[l, 2 + g].rearrange("p (kc n) -> p kc n", n=512), in_=vu[:, :, g * 512:(g + 1) * 512]), [dk("WC", l, 2 + g)])
            vd = w_down[l].rearrange("(fc p) n -> p fc n", p=128)
            for oc in range(8):
                pdma_cast(mk("dma_start",
                    out=WC_B[l, 10 + oc].rearrange("p (fc n) -> p fc n", n=128), in_=vd[:, :, oc * 128:(oc + 1) * 128]), [dk("WC", l, 10 + oc)])
            vg = w_g[l].rearrange("(kc p) n -> p kc n", p=128)
            for i in range(2):
                pdma_cast(mk("dma_start",
                    out=WC_B[l, 18 + i].rearrange("p (kc n) -> p kc n", n=512), in_=vg[:, :, i * 512:(i + 1) * 512]), [dk("WC", l, 18 + i)])
            pdma_cast(mk("dma_start",
                out=WC_B[l, 20][:, 0:2048].rearrange("p (kc n) -> p kc n", n=1024),
                in_=w_ple[l].rearrange("(kc p) n -> p kc n", p=128)), [dk("WC", l, 20)])

        emit_casts(0, "A")

        m0 = A.mark()
        COS = A.alloc(F32, [S], parts=96)
        SIN = A.alloc(F32, [S], parts=96)
        posi = A.alloc(I32, [S], parts=96)
        ang = A.alloc(F32, [S], parts=96)
        kk = A.alloc(F32, [S], parts=96)
        ki = A.alloc(I32, [S], parts=96)
        P.dma("sp", mk("dma_start", out=posi.ap, in_=pos.partition_broadcast(96)), writes=[posi])
        P.c("dve", mk("tensor_copy", out=ang.ap, in_=posi.ap), reads=[posi], writes=[ang])
        INVF = cm_f.ap[0:96, C_INVF:C_INVF + 1]
        P.c("dve", mk("tensor_scalar", out=ang.ap, in0=ang.ap, scalar1=INVF, scalar2=None, op0=ALU.mult),
            reads=[ang, cm_f], writes=[ang])
        TWO_PI = 2.0 * np.pi
        C1 = 6.28125
        C2 = TWO_PI - C1
        for (tab, shift) in ((SIN, 0.0), (COS, np.pi / 2)):
            P.c("dve", mk("tensor_scalar",
                out=tab.ap, in0=ang.ap, scalar1=float(shift), scalar2=None, op0=ALU.add), reads=[ang], writes=[tab])
            P.c("dve", mk("tensor_scalar",
                out=ki.ap, in0=tab.ap, scalar1=1.0 / TWO_PI, scalar2=None, op0=ALU.mult), reads=[tab], writes=[ki])
            P.c("dve", mk("tensor_copy", out=kk.ap, in_=ki.ap), reads=[ki], writes=[kk])
            P.c("dve", mk("scalar_tensor_tensor",
                out=tab.ap, in0=kk.ap, scalar=-C1, in1=tab.ap, op0=ALU.mult, op1=ALU.add), reads=[kk, tab], writes=[tab])
            P.c("dve", mk("scalar_tensor_tensor",
                out=tab.ap, in0=kk.ap, scalar=-C2, in1=tab.ap, op0=ALU.mult, op1=ALU.add), reads=[kk, tab], writes=[tab])
            P.c("dve", mk("tensor_scalar",
                out=tab.ap, in0=tab.ap, scalar1=-np.pi, scalar2=np.pi, op0=ALU.max, op1=ALU.min), reads=[tab], writes=[tab])
            P.c("act", mk("activation", out=tab.ap, in_=tab.ap, func=AF.Sin), reads=[tab], writes=[tab])
        P.dma("sp", mk("dma_start", out=COSD, in_=COS.ap), reads=[COS], writes=[dk("COSD")])
        P.dma("sp", mk("dma_start", out=SIND, in_=SIN.ap), reads=[SIN], writes=[dk("SIND")])
        A.reset(m0)
        const_mark = A.mark()

        def rms_rstd(ssq_ps, ssq_k, scale, parts, tmp, rstd, n=TB):
            P.c("act", mk("activation", out=tmp.ap[0:parts, 0:n], in_=ssq_ps[0:parts, 0:n], func=AF.Ln,
                                              bias=epsc.ap[0:parts], scale=float(scale)),
                reads=[ssq_k, epsc], writes=[tmp])
            P.c("act", mk("activation", out=rstd.ap[0:parts, 0:n], in_=tmp.ap[0:parts, 0:n], func=AF.Exp, scale=-0.5),
                reads=[tmp], writes=[rstd])

        def norm_block(xb, nch, gkind, l, scale, sqb, hb, tmp, rstd):
            ps, pk = nextps()
            for c in range(nch):
                sq = sqb[c % len(sqb)]
                P.c("act", mk("activation", out=sq.ap, in_=xb.ap[:, c], func=AF.Square),
                    reads=[xb.sub(c)], writes=[sq])
                P.c("pe", mk("matmul", ps[:, :], lhsT=ONES, rhs=sq.ap, start=(c == 0), stop=(c == nch - 1)),
                    reads=[sq, cm_b], writes=[pk])
            rms_rstd(ps, pk, scale, 128, tmp, rstd)
            for c in range(nch):
                P.c("dve", mk("scalar_tensor_tensor",
                    out=hb.ap[:, c], in0=xb.ap[:, c], scalar=G(gkind, l, c), in1=rstd.ap, op0=ALU.mult, op1=ALU.mult),
                    reads=[xb.sub(c), rstd, gp], writes=[hb.sub(c)])

        def head_post_gen(raw_ps, raw_k, gcolap, cs_regs, dst_ap, dst_reg, T, held_raw=False, n=TB):
            sq, qn, t1, t2, tr = T
            cosb, sinb = cs_regs
            P.c("act", mk("activation", out=sq.ap[0:96, 0:n], in_=raw_ps[0:96, 0:n], func=AF.Square),
                reads=[raw_k], writes=[sq])
            yield
            ps2, pk2 = nextps()
            P.c("pe", mk("matmul", ps2[0:96, 0:n], lhsT=BLK96, rhs=sq.ap[0:96, 0:n], start=True, stop=True),
                reads=[sq, cm_b], writes=[pk2])
            rms_rstd(ps2, pk2, 1.0, 96, tr, tr, n)
            P.c("dve", mk("scalar_tensor_tensor", out=qn.ap[0:96, 0:n], in0=raw_ps[0:96, 0:n], scalar=gcolap,
                                                        in1=tr.ap[0:96, 0:n], op0=ALU.mult, op1=ALU.mult),
                reads=[raw_k, tr, gp], writes=[qn])
            if held_raw:
                release(raw_k)
            yield
            ps3, pk3 = nextps()
            P.c("pe", mk("matmul", ps3[0:96, 0:n], lhsT=ROT, rhs=qn.ap[0:96, 0:n], start=True, stop=True),
                reads=[qn, cm_b], writes=[pk3])
            P.c("dve", mk("tensor_tensor", out=t1.ap[0:96, 0:n], in0=qn.ap[0:96, 0:n], in1=cosb.ap[0:96, 0:n], op=ALU.mult),
                reads=[qn, cosb], writes=[t1])
            P.c("dve", mk("tensor_tensor", out=t2.ap[0:96, 0:n], in0=ps3[0:96, 0:n], in1=sinb.ap[0:96, 0:n], op=ALU.mult),
                reads=[pk3, sinb], writes=[t2])
            P.c("pool", mk("tensor_tensor", out=dst_ap, in0=t1.ap[0:96, 0:n], in1=t2.ap[0:96, 0:n], op=ALU.add),
                reads=[t1, t2], writes=[dst_reg])

        def head_post(*a):
            for _ in head_post_gen(*a):
                pass

        bg = []

        def step():
            if bg:
                try:
                    next(bg[0])
                except StopIteration:
                    bg.pop(0)

        def drain():
            while bg:
                step()

        def run_chains(chains, width, tokens=None):
            tokens = {k: list(v) for k, v in (tokens or {}).items()}
            active = []
            pending = list(chains)
            finished = set()
            while active or pending:
                i = 0
                while i < len(pending) and len(active) < width:
                    c = pending[i]
                    if isinstance(c, tuple):
                        name, cls, fac, after = c
                        ok = all(a in finished for a in after) and (cls is None or tokens[cls])
                        if not ok:
                            i += 1
                            continue
                        tok = tokens[cls].pop(0) if cls is not None else None
                        active.append((fac(tok), cls, tok, name))
                    else:
                        active.append((c, None, None, None))
                    pending.pop(i)
                assert active, "chain scheduler deadlock"
                for ent in list(active):
                    g, cls, tok, name = ent
                    try:
                        next(g)
                    except StopIteration:
                        active.remove(ent)
                        if cls is not None:
                            tokens[cls].append(tok)
                        if name is not None:
                            finished.add(name)

        def xsrc(l):
            return xT if l == 0 else outT

        def xview(ap, t):
            return ap.rearrange("(c p) t -> p c t", p=128)[:, :, t * TB:(t + 1) * TB]

        out_stores = []

        for l in range(nl):
            A.reset(const_mark)
            w_in_s = A.alloc(BF16, [8, INC])
            wk_s = A.alloc(BF16, [2, 512])
            wv_s = A.alloc(BF16, [2, 512])
            wuq_s = A.alloc(BF16, [3, 768])
            xb1 = A.alloc(F32, [8, TA])
            hb2 = [A.alloc(BF16, [8, TA]) for _ in range(2)]
            sqn = A.alloc(BF16, [8, TA])
            trn = A.alloc(F32, [TA])
            zlat = A.alloc(F32, [5, TA])
            sql = A.alloc(BF16, [5, TA])
            trl = [A.alloc(F32, [TA]) for _ in range(2)]
            qln_b = A.alloc(BF16, [3, TA])
            kvn_b = A.alloc(BF16, [2, TA])
            hpT = [[A.alloc(BF16, [TA]), A.alloc(BF16, [TA]), A.alloc(F32, [TA]), A.alloc(F32, [TA]),
                    A.alloc(F32, [TA])] for _ in range(2)]
            kpe_f = A.alloc(BF16, [TA])
            qblk = A.alloc(BF16, [8, TA], parts=96)
            cs1 = A.alloc(F32, [TA])
            bs1 = A.alloc(F32, [TA])
            ubuf = [A.alloc(F32, [TA + 2]) for _ in range(4)]
            cacc1 = A.alloc(F32, [TA])
            conv_u = A.alloc(F32, [4, TA])
            conv_n = A.alloc(BF16, [4, TA])
            trc = A.alloc(F32, [TA])
            ksq2 = [A.alloc(BF16, [TA]) for _ in range(2)]
            ktr2 = [A.alloc(F32, [TA]) for _ in range(2)]
            ktb = A.alloc(BF16, [8, TA], parts=96)
            vb = A.alloc(BF16, [TA // 128, 8 * 65])
            cosb2 = [A.alloc(F32, [TA], parts=96) for _ in range(2)]
            sinb2 = [A.alloc(F32, [TA], parts=96) for _ in range(2)]

            P.dma("sp", mk("dma_start", out=w_in_s.ap, in_=WIN_B[l].rearrange("p (kc n) -> p kc n", n=INC)),
                  reads=[dk("WIN", l, kc) for kc in range(8)], writes=[w_in_s])
            P.dma("sp", mk("dma_start", out=wk_s.ap, in_=WK_B[l].rearrange("p (kc n) -> p kc n", n=512)),
                  reads=[dk("WK", l, 0), dk("WK", l, 1)], writes=[wk_s])
            P.dma("sp", mk("dma_start", out=wv_s.ap, in_=WV_B[l].rearrange("p (kc n) -> p kc n", n=512)),
                  reads=[dk("WV", l, 0), dk("WV", l, 1)], writes=[wv_s])
            P.dma("sp", mk("dma_start", out=wuq_s.ap, in_=WUQ_B[l].rearrange("p (kc n) -> p kc n", n=768)),
                  reads=[dk("WUQ", l)], writes=[wuq_s])
            vb4 = vb.ap.rearrange("p a (h x) -> p a h x", x=65)
            P.c("pool", mk("memset", vb.ap, 1.0), writes=[vb])
            for cc in range(4):
                P.c("pool", mk("memset", ubuf[cc].ap[:, 0:2], 0.0), writes=[ubuf[cc]])

            def xviewA(ap, t):
                return ap.rearrange("(c p) t -> p c t", p=128)[:, :, t * TA:(t + 1) * TA]

            def loadcs(t):
                tc_ = slice(t * TA, (t + 1) * TA)
                P.dma("sp", mk("dma_start", out=cosb2[t % 2].ap, in_=COSD[:, tc_]), reads=[dk("COSD")], writes=[cosb2[t % 2]])
                P.dma("sp", mk("dma_start", out=sinb2[t % 2].ap, in_=SIND[:, tc_]), reads=[dk("SIND")], writes=[sinb2[t % 2]])

            def loadxA(t):
                xb = xb1
                P.dma("sp", mk("dma_start", out=xb.ap, in_=xviewA(xsrc(l), t)), reads=[dk("X", (t * TA) // TB)], writes=[xb])

            def c_normx(t):
                xb = xb1
                hbuf = hb2[t % 2]
                for c in range(8):
                    P.c("act", mk("activation", out=sqn.ap[:, c], in_=xb.ap[:, c], func=AF.Square),
                        reads=[xb.sub(c)], writes=[sqn.sub(c)])
                yield
                ps, pk = nextps()
                for c in range(8):
                    P.c("pe", mk("matmul", ps[:, 0:TA], lhsT=ONES, rhs=sqn.ap[:, c], start=(c == 0), stop=(c == 7)),
                        reads=[sqn.sub(c), cm_b], writes=[pk])
                rms_rstd(ps, pk, 1.0 / D, 128, trn, trn, TA)
                yield
                for c in range(8):
                    P.c("dve", mk("scalar_tensor_tensor",
                        out=hbuf.ap[:, c], in0=xb.ap[:, c], scalar=G(G_MIX, l, c), in1=trn.ap, op0=ALU.mult, op1=ALU.mult),
                        reads=[xb.sub(c), trn, gp], writes=[hbuf.sub(c)])
                if t + 1 < NBA:
                    loadxA(t + 1)

            def zmm(hbuf, ps, pk, c0, m):
                for kc in range(8):
                    P.c("pe", mk("matmul", ps[0:m, 0:TA], lhsT=w_in_s.ap[:, kc, c0:c0 + m], rhs=hbuf.ap[:, kc],
                                 start=(kc == 0), stop=(kc == 7)),
                        reads=[w_in_s, hbuf.sub(kc)], writes=[pk])

            def c_lat(hbuf, which):
                c0, nch, zoff, gk, scale, dstb = ((0, 3, 0, G_QL, 1.0 / 384, qln_b), (384, 2, 3, G_KVL, 1.0 / 256, kvn_b))[which]
                tmp = rstd = trl[which]
                for c in range(nch):
                    ps, pk = nextps()
                    zmm(hbuf, ps, pk, c0 + c * 128, 128)
                    P.c("act", mk("activation", out=zlat.ap[:, zoff + c], in_=ps[:, 0:TA], func=AF.Copy),
                        reads=[pk], writes=[zlat.sub(zoff + c)])
                    P.c("act", mk("activation", out=sql.ap[:, zoff + c], in_=ps[:, 0:TA], func=AF.Square),
                        reads=[pk], writes=[sql.sub(zoff + c)])
                    yield
                ps, pk = nextps()
                for c in range(nch):
                    P.c("pe", mk("matmul", ps[:, 0:TA], lhsT=ONES, rhs=sql.ap[:, zoff + c], start=(c == 0), stop=(c == nch - 1)),
                        reads=[sql.sub(zoff + c), cm_b], writes=[pk])
                rms_rstd(ps, pk, scale, 128, tmp, rstd, TA)
                yield
                for c in range(nch):
                    P.c("dve", mk("scalar_tensor_tensor",
                        out=dstb.ap[:, c], in0=zlat.ap[:, zoff + c], scalar=G(gk, l, c), in1=rstd.ap, op0=ALU.mult, op1=ALU.mult),
                        reads=[zlat.sub(zoff + c), rstd, gp], writes=[dstb.sub(c)])

            def c_kpe(hbuf, tcols, tok):
                ps, pk = nextps(hold=True)
                zmm(hbuf, ps, pk, 576, 96)
                yield from head_post_gen(ps, pk, G(G_KR, l, 0, 0, 96), tcols, kpe_f.ap[0:96], kpe_f, hpT[tok], True, TA)

            def c_qh(h, tcols, tok):
                ps, pk = nextps(hold=True)
                for kc in range(3):
                    P.c("pe", mk("matmul", ps[0:96, 0:TA], lhsT=wuq_s.ap[:, kc, h * 96:(h + 1) * 96], rhs=qln_b.ap[:, kc],
                                 start=(kc == 0), stop=(kc == 2)),
                        reads=[wuq_s, qln_b.sub(kc)], writes=[pk])
                yield from head_post_gen(ps, pk, G(G_QN, l, 0, 0, 96), tcols, qblk.ap[0:96, h], qblk.sub(h), hpT[tok], True, TA)

            def c_conv(hbuf, cc):
                cs, bs, cacc, ub = cs1, bs1, cacc1, ubuf[cc]
                psb, pkb = nextps()
                zmm(hbuf, psb, pkb, 672 + cc * 128, 128)
                P.c("act", mk("activation", out=bs.ap, in_=psb[:, 0:TA], func=AF.Copy), reads=[pkb], writes=[bs])
                yield
                psc, pkc = nextps()
                zmm(hbuf, psc, pkc, 1184 + cc * 128, 128)
                P.c("act", mk("activation", out=cs.ap, in_=psc[:, 0:TA], func=AF.Copy), reads=[pkc], writes=[cs])
                yield
                psx, pkx = nextps()
                zmm(hbuf, psx, pkx, 1696 + cc * 128, 128)
                P.c("dve", mk("tensor_tensor", out=ub.ap[:, 2:TA + 2], in0=cs.ap, in1=psx[:, 0:TA], op=ALU.mult),
                    reads=[cs, pkx], writes=[ub])
                yield
                P.c("act", mk("activation", out=cacc.ap, in_=ub.ap[:, 2:TA + 2], func=AF.Copy, scale=G(G_CW, l, 0 * 4 + cc)),
                    reads=[ub, gp], writes=[cacc])
                P.c("dve", mk("scalar_tensor_tensor", out=cacc.ap, in0=ub.ap[:, 1:TA + 1], scalar=G(G_CW, l, 1 * 4 + cc),
                              in1=cacc.ap, op0=ALU.mult, op1=ALU.add),
                    reads=[ub, gp, cacc], writes=[cacc])
                P.c("dve", mk("scalar_tensor_tensor", out=cacc.ap, in0=ub.ap[:, 0:TA], scalar=G(G_CW, l, 2 * 4 + cc),
                              in1=cacc.ap, op0=ALU.mult, op1=ALU.add),
                    reads=[ub, gp, cacc], writes=[cacc])
                P.c("pool", mk("tensor_tensor", out=conv_u.ap[:, cc], in0=cacc.ap, in1=bs.ap, op=ALU.mult),
                    reads=[cacc, bs], writes=[conv_u.sub(cc)])
                P.c("pool", mk("tensor_copy", out=ub.ap[:, 0:2], in_=ub.ap[:, TA:TA + 2]), reads=[ub], writes=[ub])

            def c_convnorm(t, tcols):
                for cc in range(4):
                    P.c("act", mk("activation", out=sqn.ap[:, cc], in_=conv_u.ap[:, cc], func=AF.Square),
                        reads=[conv_u.sub(cc)], writes=[sqn.sub(cc)])
                yield
                ps, pk = nextps()
                for cc in range(4):
                    P.c("pe", mk("matmul", ps[:, 0:TA], lhsT=ONES, rhs=sqn.ap[:, cc], start=(cc == 0), stop=(cc == 3)),
                        reads=[sqn.sub(cc), cm_b], writes=[pk])
                rms_rstd(ps, pk, 1.0 / 512, 128, trc, trc, TA)
                yield
                for cc in range(4):
                    P.c("dve", mk("scalar_tensor_tensor",
                        out=conv_n.ap[:, cc], in0=conv_u.ap[:, cc], scalar=G(G_OC, l, cc), in1=trc.ap, op0=ALU.mult, op1=ALU.mult),
                        reads=[conv_u.sub(cc), trc, gp], writes=[conv_n.sub(cc)])
                P.dma("sp", mk("dma_start", out=MIXC[:, :, tcols], in_=conv_n.ap), reads=[conv_n], writes=[dk("MIXC", t)])

            def c_kpair(jp, k):
                ksq = ksq2[k]
                ktmp = krstd = ktr2[k]
                ps, pk = nextps(hold=True)
                for kc in range(2):
                    P.c("pe", mk("matmul", ps[:, 0:TA], lhsT=wk_s.ap[:, kc, jp * 128:(jp + 1) * 128], rhs=kvn_b.ap[:, kc],
                                 start=(kc == 0), stop=(kc == 1)),
                        reads=[wk_s, kvn_b.sub(kc)], writes=[pk])
                P.c("act", mk("activation", out=ksq.ap, in_=ps[:, 0:TA], func=AF.Square), reads=[pk], writes=[ksq])
                yield
                ps2, pk2 = nextps()
                P.c("pe", mk("matmul", ps2[:, 0:TA], lhsT=BLK2, rhs=ksq.ap, start=True, stop=True),
                    reads=[ksq, cm_b], writes=[pk2])
                rms_rstd(ps2, pk2, 1.0, 128, ktmp, krstd, TA)
                yield
                for hh in range(2):
                    lo, hi = hh * 64, hh * 64 + 64
                    P.c("dve", mk("scalar_tensor_tensor",
                        out=ktb.ap[0:64, 2 * jp + hh], in0=ps[lo:hi, 0:TA], scalar=G(G_KN2, l, 0, lo, hi), in1=krstd.ap[lo:hi],
                        op0=ALU.mult, op1=ALU.mult),
                        reads=[pk, krstd, gp], writes=[ktb.sub(2 * jp + hh)])
                release(pk)

            def c_v(t):
                for i in range(TA // 128):
                    ps, pk = nextps()
                    for kc in range(2):
                        P.c("pe", mk("matmul", ps[:, :], lhsT=kvn_b.ap[:, kc, i * 128:(i + 1) * 128], rhs=wv_s.ap[:, kc],
                                     start=(kc == 0), stop=(kc == 1)),
                            reads=[wv_s, kvn_b.sub(kc)], writes=[pk])
                    P.c("act", mk("activation", out=vb4[:, i, :, 0:64], in_=ps[:, :].rearrange("p (h x) -> p h x", x=64), func=AF.Copy),
                        reads=[pk], writes=[vb.sub(i)])
                    yield
                nt = TA // 128
                P.dma("sp", mk("dma_start", out=VVD[:, nt * t:nt * t + nt, :], in_=vb.ap), reads=[vb], writes=[dk("VV", t)])

            def c_kfin(t, tcols):
                for h in range(8):
                    P.c("pool", mk("tensor_copy", out=ktb.ap[64:96, h], in_=kpe_f.ap[64:96]), reads=[kpe_f], writes=[ktb.sub(h)])
                P.dma("sp", mk("dma_start", out=KTD[:, :, tcols], in_=ktb.ap), reads=[ktb], writes=[dk("KT", t)])
                yield

            def c_qfin(t, tcols):
                P.dma("sp", mk("dma_start", out=QTD[:, :, tcols], in_=qblk.ap), reads=[qblk], writes=[dk("QT", t)])
                yield

            loadxA(0)
            loadcs(0)
            run_chains([c_normx(0)], 1)
            for t in range(NBA):
                tcols = (cosb2[t % 2], sinb2[t % 2])
                tsl = slice(t * TA, (t + 1) * TA)
                if t + 1 < NBA:
                    loadcs(t + 1)
                hbuf = hb2[t % 2]
                def QH(h):
                    return ("qh%d" % h, "hp", lambda tok, h=h: c_qh(h, tcols, tok), ["latq"])

                def KP(jp):
                    return ("kp%d" % jp, "ks", lambda tok, jp=jp: c_kpair(jp, tok), ["latkv"])

                def CV(cc):
                    return ("cv%d" % cc, "cv", lambda tok, cc=cc: c_conv(hbuf, cc), [])

                chains = [("latq", None, lambda tok: c_lat(hbuf, 0), []), ("latkv", None, lambda tok: c_lat(hbuf, 1), []),
                          ("kpe", "hp", lambda tok: c_kpe(hbuf, tcols, tok), []), CV(0)]
                if t + 1 < NBA:
                    chains.append(("normx", "sqn", lambda tok: c_normx(t + 1), []))
                chains += [CV(1), QH(0), KP(0), QH(1), CV(2), KP(1), QH(2), QH(3), CV(3), KP(2), QH(4), KP(3),
                           QH(5), ("v", None, lambda tok: c_v(t), ["latkv"]), QH(6),
                           ("convnorm", "sqn", lambda tok: c_convnorm(t, tsl), ["cv0", "cv1", "cv2", "cv3"]), QH(7)]
                run_chains(chains, WCH, {"hp": list(range(K_HP)), "ks": list(range(K_KS)), "cv": [0], "sqn": [0]})
                run_chains([c_kfin(t, tsl), c_qfin(t, tsl)], 2)
                if l == 0 and t == 0:
                    emit_casts(0, "C")

            A.reset(const_mark)
            kt_s = A.alloc(BF16, [8, S], parts=96)
            vv_s = A.alloc(BF16, [32, 8 * 65])
            qblk2 = [A.alloc(BF16, [8, TB], parts=96) for _ in range(2)]
            ptb = [A.alloc(BF16, [TB]) for _ in range(K_LA + 2)]
            osb = A.alloc(F32, [TB])
            rden = A.alloc(F32, [TB])
            attn_u = A.alloc(F32, [4, TB])
            attn_n = A.alloc(BF16, [4, TB])
            sqB = [A.alloc(BF16, [TB]) for _ in range(4)]
            tmpB = A.alloc(F32, [TB])
            rstdB = A.alloc(F32, [TB])
            vv4 = vv_s.ap.rearrange("p a (h x) -> p a h x", x=65)

            def loadq(j):
                qb = qblk2[j % 2]
                P.dma("sp", mk("dma_start", out=qb.ap, in_=QTD[:, :, j * TB:(j + 1) * TB]), reads=[dk("QT", i) for i in range(j * RAB, (j + 1) * RAB)], writes=[qb])

            def loadkv(j):
                P.dma("sp", mk("dma_start", out=kt_s.ap[:, :, j * TB:(j + 1) * TB], in_=KTD[:, :, j * TB:(j + 1) * TB]),
                      reads=[dk("KT", i) for i in range(j * RAB, (j + 1) * RAB)], writes=[kt_s] if j == 0 else [[("kt", j)]])
                P.dma("sp", mk("dma_start", out=vv_s.ap[:, 4 * j:4 * j + 4], in_=VVD[:, 4 * j:4 * j + 4, :]),
                      reads=[dk("VV", i) for i in range(j * RAB, (j + 1) * RAB)], writes=[vv_s] if j == 0 else [[("vv", j)]])

            loadq(0)
            loadkv(0)
            if l + 1 < nl:
                defer_casts[0] = True
                emit_casts(l + 1, "A")
                emit_casts(l + 1, "C")
                defer_casts[0] = False
            SCALE = 96.0 ** -0.5

            def tail_chain(h, pso, pko, j=0):
                P.c("dve", mk("tensor_copy", out=osb.ap[0:65], in_=pso[0:65, :]), reads=[pko], writes=[osb])
                release(pko)
                if j >= 2:
                    P.c("dve", mk("reciprocal", out=rden.ap[64:65], in_=osb.ap[64:65]), reads=[osb], writes=[rden])
                    yield
                else:
                    P.c("act", mk("activation", out=rden.ap[64:65], in_=osb.ap[64:65], func=AF.Ln), reads=[osb], writes=[rden])
                    P.c("act", mk("activation", out=rden.ap[64:65], in_=rden.ap[64:65], func=AF.Exp, scale=-1.0), reads=[rden], writes=[rden])
                yield
                psb_, pkb_ = nextps()
                P.c("pe", mk("matmul", psb_[0:64, :], lhsT=ONESF[64:65, 0:64], rhs=rden.ap[64:65], start=True, stop=True),
                    reads=[rden, cm_f], writes=[pkb_])
                plo = (h % 2) * 64
                P.c("dve", mk("tensor_tensor", out=attn_u.ap[plo:plo + 64, h // 2], in0=osb.ap[0:64], in1=psb_[0:64, :], op=ALU.mult),
                    reads=[osb, pkb_], writes=[attn_u.sub(h // 2)])

            def block_tail(j):
                for c in range(4):
                    P.c("pool", mk("tensor_tensor", out=sqB[c].ap, in0=attn_u.ap[:, c], in1=attn_u.ap[:, c], op=ALU.mult),
                        reads=[attn_u.sub(c)], writes=[sqB[c]])
                yield
                yield
                ps, pk = nextps()
                for c in range(4):
                    P.c("pe", mk("matmul", ps[:, :], lhsT=ONES, rhs=sqB[c].ap, start=(c == 0), stop=(c == 3)),
                        reads=[sqB[c], cm_b], writes=[pk])
                rms_rstd(ps, pk, 1.0 / 512, 128, tmpB, rstdB)
                yield
                for c in range(4):
                    P.c("dve", mk("scalar_tensor_tensor",
                        out=attn_n.ap[:, c], in0=attn_u.ap[:, c], scalar=G(G_OA, l, c), in1=rstdB.ap,
                        op0=ALU.mult, op1=ALU.mult),
                        reads=[attn_u.sub(c), rstdB, gp], writes=[attn_n.sub(c)])
                P.dma("sp", mk("dma_start", out=MIXA[:, :, j * TB:(j + 1) * TB], in_=attn_n.ap), reads=[attn_n], writes=[dk("MIXA", j)])

            for j in range(NB):
                if j + 1 < NB:
                    loadq(j + 1)
                    loadkv(j + 1)
                nkc = 4 * j + 4
                qb = qblk2[j % 2]
                tasks = [(h, kc) for h in range(8) for kc in range(nkc)]
                stiles = {}
                psos = {}
                LA = K_LA

                def issue_s(i):
                    h, kc = tasks[i]
                    r = kc - 4 * j
                    qoff = 128 * r if r > 0 else 0
                    n = TB - qoff
                    pss, pks = nextps(hold=True)
                    kdeps = [kt_s] if kc < 4 else [[("kt", kc // 4)]]
                    P.c("pe", mk("matmul",
                        pss[:, 0:n], lhsT=kt_s.ap[:, h, kc * 128:(kc + 1) * 128], rhs=qb.ap[0:96, h, qoff:TB], start=True, stop=True),
                        reads=kdeps + [qb.sub(h)], writes=[pks])
                    stiles[i] = (pss, pks, r, qoff, n)

                for i in range(min(LA, len(tasks))):
                    issue_s(i)
                for i, (h, kc) in enumerate(tasks):
                    if kc == 0:
                        if j >= 1:
                            flush_casts(1 if j < NB - 1 else 1000)
                        while len(held) - len(stiles) > 1:
                            step()
                        psos[h] = nextps(hold=True)
                    pso, pko = psos[h]
                    pss, pks, r, qoff, n = stiles.pop(i)
                    pt = ptb[i % (K_LA + 2)]
                    vdeps = [vv_s] if kc < 4 else [[("vv", kc // 4)]]
                    P.c("act", mk("activation", out=pt.ap[:, 0:n], in_=pss[:, 0:n], func=AF.Exp, scale=SCALE),
                        reads=[pks], writes=[pt])
                    release(pks)
                    if r >= 0:
                        P.c("pool", mk("tensor_tensor", out=pt.ap[:, 0:128], in0=pt.ap[:, 0:128], in1=TRI, op=ALU.mult),
                            reads=[pt, cm_b], writes=[pt])
                    if i + LA < len(tasks):
                        issue_s(i + LA)
                    P.c("pe", mk("matmul",
                        pso[0:65, qoff:TB], lhsT=vv4[:, kc, h, :], rhs=pt.ap[:, 0:n], start=(kc == 0), stop=(kc == nkc - 1)),
                        reads=vdeps + [pt], writes=[pko])
                    if i % K_STEP == K_STEP - 1:
                        step()
                    if kc == nkc - 1:
                        if h == 7:
                            drain()
                        bg.append(tail_chain(h, pso, pko, j))
                        if h == 7:
                            bg.append(block_tail(j))
            drain()

            A.reset(const_mark)
            NRING = 6
            ring = [A.alloc(BF16, [PIECE]) for _ in range(NRING)]
            xc2 = [A.alloc(F32, [8, TB]) for _ in range(2)]
            hC = A.alloc(BF16, [8, TB])
            hCb = A.alloc(BF16, [8, TB])
            sqC = [A.alloc(BF16, [TB]) for _ in range(3)]
            uC = A.alloc(BF16, [32, TB])
            mixa2 = [A.alloc(BF16, [4, TB]) for _ in range(2)]
            mixc2 = [A.alloc(BF16, [4, TB]) for _ in range(2)]
            pb2 = [A.alloc(BF16, [2, TB]) for _ in range(2)]
            gate = A.alloc(F32, [TB])
            gtmp = A.alloc(F32, [TB])
            relu2 = [A.alloc(F32, [TB]) for _ in range(2)]
            tmpC = A.alloc(F32, [TB])
            tmpC3 = A.alloc(F32, [TB])
            rstdC3 = A.alloc(F32, [TB])
            rstdC = A.alloc(F32, [TB])
            ringctr = [0]

            def loadpiece(i):
                slot = ring[ringctr[0] % NRING]
                ringctr[0] += 1
                if i == 20:
                    P.dma("sp", mk("dma_start", out=slot.ap[:, 0:2048], in_=WC_B[l, i][:, 0:2048]), reads=[dk("WC", l, i)], writes=[slot])
                else:
                    P.dma("sp", mk("dma_start", out=slot.ap, in_=WC_B[l, i]), reads=[dk("WC", l, i)], writes=[slot])
                return slot

            def loadblk(t):
                xb = xc2[t % 2]
                P.dma("sp", mk("dma_start", out=xb.ap, in_=xview(xsrc(l), t)), reads=[dk("X", t)], writes=[xb])
                ma = mixa2[t % 2]
                P.dma("sp", mk("dma_start", out=ma.ap, in_=MIXA[:, :, t * TB:(t + 1) * TB]), reads=[dk("MIXA", t)], writes=[ma])
                mc = mixc2[t % 2]
                P.dma("sp", mk("dma_start", out=mc.ap, in_=MIXC[:, :, t * TB:(t + 1) * TB]), reads=[dk("MIXC", i) for i in range(t * RAB, (t + 1) * RAB)], writes=[mc])
                pb = pb2[t % 2]
                P.dma("pool", mk("dma_start", out=pb.ap, in_=pT[l].rearrange("(kc p) t -> p kc t", p=128)[:, :, t * TB:(t + 1) * TB]),
                      writes=[pb])

            W_P, U_P, D_P, G_P = [0, 1], list(range(2, 10)), list(range(10, 18)), [18, 19, 20]
            sched = [(0, i) for i in W_P]
            for t in range(NB):
                sched += [(t, i) for i in U_P] + [(t, i) for i in D_P]
                if t + 1 < NB:
                    sched += [(t + 1, i) for i in W_P]
                sched += [(t, i) for i in G_P]
            pos = {k: n for n, k in enumerate(sched)}
            loaded = {}
            nload = [0]

            def ensure(upto):
                while nload[0] < len(sched) and nload[0] <= upto:
                    tt, ii = sched[nload[0]]
                    loaded[(tt, ii)] = loadpiece(ii)
                    nload[0] += 1

            def piece(t, i):
                ensure(pos[(t, i)])
                return loaded[(t, i)]

            def done(t, i):
                ensure(pos[(t, i)] + NRING)

            hC2 = [hC, hCb]

            def nacc_begin():
                nps, npk = nextps(hold=True)
                return {"ps": nps, "pk": npk}

            def nacc_mm(st, c):
                sq = sqC[c % 3]
                P.c("pe", mk("matmul", st["ps"][:, :], lhsT=ONES, rhs=sq.ap, start=(c == 0), stop=(c == 7)),
                    reads=[sq, cm_b], writes=[st["pk"]])

            def nacc_chunk(st, xb, c):
                sq = sqC[c % 3]
                P.c("act", mk("activation", out=sq.ap, in_=xb.ap[:, c], func=AF.Square), reads=[xb.sub(c)], writes=[sq])

            def nacc_stats(st, tmp, rstd):
                nacc_mm(st, 7)
                rms_rstd(st["ps"], st["pk"], 1.0 / D, 128, tmp, rstd)
                release(st["pk"])

            def norm_apply(xb, gkind, hdst, rstd):
                for c in range(8):
                    P.c("dve", mk("scalar_tensor_tensor",
                        out=hdst.ap[:, c], in0=xb.ap[:, c], scalar=G(gkind, l, c), in1=rstd.ap, op0=ALU.mult, op1=ALU.mult),
                        reads=[xb.sub(c), rstd, gp], writes=[hdst.sub(c)])

            def sec_W(t, defer_h3=None):
                xb, ma, mc = xc2[t % 2], mixa2[t % 2], mixc2[t % 2]
                po0, po1 = piece(t, 0), piece(t, 1)
                wo0 = po0.ap.rearrange("p (c n) -> p c n", n=1024)
                wo1 = po1.ap.rearrange("p (c n) -> p c n", n=1024)
                nst = nacc_begin()
                for oc in range(8):
                    ps, pk = nextps()
                    ocs = slice(oc * 128, (oc + 1) * 128)
                    for c in range(4):
                        P.c("pe", mk("matmul", ps[:, :], lhsT=wo0[:, c, ocs], rhs=ma.ap[:, c], start=(c == 0), stop=False),
                            reads=[po0, ma.sub(c)], writes=[pk])
                    for c in range(4):
                        P.c("pe", mk("matmul", ps[:, :], lhsT=wo1[:, c, ocs], rhs=mc.ap[:, c], start=False, stop=(c == 3)),
                            reads=[po1, mc.sub(c)], writes=[pk])
                    if oc >= 1:
                        nacc_mm(nst, oc - 1)
                    P.c("dve", mk("tensor_tensor", out=xb.ap[:, oc], in0=xb.ap[:, oc], in1=ps[:, :], op=ALU.add),
                        reads=[xb.sub(oc), pk], writes=[xb.sub(oc)])
                    nacc_chunk(nst, xb, oc)
                done(t, 0)
                done(t, 1)
                if defer_h3 is not None:
                    norm_apply(defer_h3[0], G_PLE, hC2[1], defer_h3[1])
                nacc_stats(nst, tmpC, rstdC)
                norm_apply(xb, G_MLP, hC2[0], rstdC)

            def sec_U(t):
                h2 = hC2[0]
                for g in range(8):
                    pc = piece(t, 2 + g)
                    wup = pc.ap.rearrange("p (kc n) -> p kc n", n=512)
                    for f in range(4):
                        ps, pk = nextps()
                        for kc in range(8):
                            P.c("pe", mk("matmul", ps[:, :], lhsT=wup[:, kc, f * 128:(f + 1) * 128], rhs=h2.ap[:, kc],
                                         start=(kc == 0), stop=(kc == 7)),
                                reads=[pc, h2.sub(kc)], writes=[pk])
                        rl = relu2[(4 * g + f) % 2]
                        P.c("act", mk("activation", out=rl.ap, in_=ps[:, :], func=AF.Relu), reads=[pk], writes=[rl])
                        P.c("dve", mk("tensor_tensor", out=uC.ap[:, 4 * g + f], in0=rl.ap, in1=rl.ap, op=ALU.mult),
                            reads=[rl], writes=[uC.sub(4 * g + f)])
                    done(t, 2 + g)

            def sec_D(t):
                xb = xc2[t % 2]
                nst3 = nacc_begin()
                for oc in range(8):
                    pc = piece(t, 10 + oc)
                    wdn = pc.ap.rearrange("p (fc n) -> p fc n", n=128)
                    ps, pk = nextps()
                    for fc in range(32):
                        P.c("pe", mk("matmul", ps[:, :], lhsT=wdn[:, fc, :], rhs=uC.ap[:, fc], start=(fc == 0), stop=(fc == 31)),
                            reads=[pc, uC.sub(fc)], writes=[pk])
                    if oc >= 1:
                        nacc_mm(nst3, oc - 1)
                    P.c("dve", mk("tensor_tensor", out=xb.ap[:, oc], in0=xb.ap[:, oc], in1=ps[:, :], op=ALU.add),
                        reads=[xb.sub(oc), pk], writes=[xb.sub(oc)])
                    nacc_chunk(nst3, xb, oc)
                    done(t, 10 + oc)
                nacc_stats(nst3, tmpC3, rstdC3)

            def sec_G(t):
                xb, pb, h3 = xc2[t % 2], pb2[t % 2], hC2[1]
                pg = [piece(t, 18), piece(t, 19)]
                pp = piece(t, 20)
                wpl = pp.ap[:, 0:2048].rearrange("p (kc n) -> p kc n", n=1024)
                for oc in range(8):
                    pgi = pg[oc // 4]
                    wgv = pgi.ap.rearrange("p (kc n) -> p kc n", n=512)
                    o4 = (oc % 4) * 128
                    ps, pk = nextps()
                    for kc in range(8):
                        P.c("pe", mk("matmul", ps[:, :], lhsT=wgv[:, kc, o4:o4 + 128], rhs=h3.ap[:, kc],
                                     start=(kc == 0), stop=(kc == 7)),
                            reads=[pgi, h3.sub(kc)], writes=[pk])
                    P.c("act", mk("activation", out=gate.ap, in_=ps[:, :], func=AF.Sigmoid), reads=[pk], writes=[gate])
                    ps2, pk2 = nextps()
                    for kc in range(2):
                        P.c("pe", mk("matmul", ps2[:, :], lhsT=wpl[:, kc, oc * 128:(oc + 1) * 128], rhs=pb.ap[:, kc],
                                     start=(kc == 0), stop=(kc == 1)),
                            reads=[pp, pb.sub(kc)], writes=[pk2])
                    P.c("dve", mk("tensor_tensor", out=gtmp.ap, in0=gate.ap, in1=ps2[:, :], op=ALU.mult),
                        reads=[gate, pk2], writes=[gtmp])
                    P.c("pool", mk("tensor_tensor", out=xb.ap[:, oc], in0=xb.ap[:, oc], in1=gtmp.ap, op=ALU.add),
                        reads=[xb.sub(oc), gtmp], writes=[xb.sub(oc)])
                done(t, 18)
                done(t, 19)
                done(t, 20)
                so = P.dma("sp", mk("dma_start", out=xview(outT, t), in_=xb.ap), reads=[xb], writes=[dk("X", t)])
                if l == nl - 1:
                    out_stores.append(so)

            loadblk(0)
            if NB > 1:
                loadblk(1)
            ensure(NRING - 1)
            sec_W(0)
            for t in range(NB):
                sec_U(t)
                sec_D(t)
                if t + 1 < NB:
                    sec_W(t + 1, defer_h3=(xc2[t % 2], rstdC3))
                else:
                    norm_apply(xc2[t % 2], G_PLE, hC2[1], rstdC3)
                sec_G(t)
                if t + 2 < NB:
                    loadblk(t + 2)

        P.emit(final_ops=out_stores)
        build.stats = P.stats
    return nc


def _gpack(inp):
    g = np.zeros((128, NG), np.float32)
    for l in range(NL):
        for kind, name in ((G_MIX, "g_mix"), (G_MLP, "g_mlp"), (G_PLE, "g_ple")):
            for c in range(8):
                g[:, gcol(kind, l, c)] = inp[name][l, c * 128:(c + 1) * 128]
        for c in range(3):
            g[:, gcol(G_QL, l, c)] = inp["g_q_lat"][l, c * 128:(c + 1) * 128]
        for c in range(2):
            g[:, gcol(G_KVL, l, c)] = inp["g_kv_lat"][l, c * 128:(c + 1) * 128]
        for c in range(4):
            g[:, gcol(G_OC, l, c)] = inp["g_out_conv"][l, c * 128:(c + 1) * 128]
        for j in range(3):
            for c in range(4):
                g[:, gcol(G_CW, l, j * 4 + c)] = inp["conv_w"][l, j, c * 128:(c + 1) * 128]
        for c in range(4):
            g[:, gcol(G_OA, l, c)] = inp["g_out_attn"][l, c * 128:(c + 1) * 128]
        g[0:64, gcol(G_QN, l)] = inp["g_qn_nope"][l]
        g[64:96, gcol(G_QN, l)] = inp["g_qn_rope"][l]
        g[0:64, gcol(G_KN2, l)] = inp["g_kn_nope"][l]
        g[64:128, gcol(G_KN2, l)] = inp["g_kn_nope"][l]
        g[64:96, gcol(G_KR, l)] = inp["g_kn_rope"][l]
    return g


def _cmat():
    c = np.zeros((128, NCM), np.float32)
    c[:, C_ONES:C_ONES + 128] = 1.0
    c[0:64, C_BLK96:C_BLK96 + 64] = 1.0 / 64
    c[64:96, C_BLK96 + 64:C_BLK96 + 96] = 1.0 / 32
    c[0:64, C_BLK2:C_BLK2 + 64] = 1.0 / 64
    c[64:128, C_BLK2 + 64:C_BLK2 + 128] = 1.0 / 64
    for i in range(16):
        c[80 + i, C_ROT + 64 + i] = -1.0
        c[64 + i, C_ROT + 80 + i] = 1.0
    kq = np.arange(128)
    c[:, C_TRI:C_TRI + 128] = (kq[None, :] >= kq[:, None]).astype(np.float32)
    f = (1.0 / (10000.0 ** (np.arange(0, 32, 2, dtype=np.float32) / 32))).astype(np.float32)
    c[64:80, C_INVF] = f
    c[80:96, C_INVF] = f
    return c


_NC_CACHE = {}


def kernel(**inp):
    inp = {k: np.asarray(v) for k, v in inp.items()}
    if "nc" not in _NC_CACHE:
        _NC_CACHE["nc"] = build()
    nc = _NC_CACHE["nc"]
    gp = _gpack(inp)
    cm = _cmat()
    shared = {k: np.ascontiguousarray(inp[k], dtype=np.float32) for k in
              ("w_in", "w_uq", "w_ukv", "w_o", "w_up", "w_down", "w_ple_gate", "w_ple")}
    in_maps = []
    for b in range(8):
        m = dict(shared)
        m["xT"] = np.ascontiguousarray(inp["x"][b].T)
        m["pT"] = np.ascontiguousarray(np.transpose(inp["p"][:, b], (0, 2, 1)))
        m["pos"] = np.ascontiguousarray(inp["positions"][b][None, :].astype(np.int32))
        m["gpack"] = gp
        m["cmat"] = cm
        in_maps.append(m)
    res = run_bass_kernel_spmd(nc, in_maps, core_ids=list(range(8)))
    out = np.stack([np.ascontiguousarray(r["outT"].T) for r in res.results], axis=0)
    return out.astype(np.float32)
```
